# Optimizing a Trainium2 kernel written in Bass

```python
import jax, jax.numpy as jnp
from jax import lax
import numpy as np

D_MODEL = 1024
BATCH = 16
SEQ = 2048
DEPTH = 1

RMS_EPS = 1e-6
SSD_D_INNER = 2 * D_MODEL
SSD_HEADDIM = 64
SSD_N_HEADS = SSD_D_INNER // SSD_HEADDIM
SSD_N_GROUPS = 4
SSD_D_STATE = 128
SSD_CONV = 4
SSD_CHUNK = 128
SSD_CONV_DIM = SSD_D_INNER + 2 * SSD_N_GROUPS * SSD_D_STATE
MLSTM_N_HEADS = 4
MLSTM_D_V = 2 * D_MODEL
MLSTM_D_QK = D_MODEL
MLSTM_HEAD_V = MLSTM_D_V // MLSTM_N_HEADS
MLSTM_HEAD_QK = MLSTM_D_QK // MLSTM_N_HEADS
MLSTM_CONV = 4
MLSTM_CHUNK = 128
D_FF = -(-(8 * D_MODEL) // (3 * 256)) * 256
IN_SPLITS = (SSD_D_INNER,
             SSD_CONV_DIM,
             SSD_N_HEADS,
             2 * MLSTM_D_QK,
             MLSTM_D_V,
             MLSTM_D_V,
             2 * MLSTM_N_HEADS,
             2 * D_MODEL)
D_IN_PROJ = sum(IN_SPLITS)

kernel_name = "hybrid_ssd_mlstm_gated_adaln_block"


def split_sizes(x, sizes):
    return jnp.split(x, np.cumsum(sizes)[:-1].tolist(), axis=-1)


def rms_norm(x, eps=RMS_EPS):
    xf = x.astype(jnp.float32)
    return (xf * lax.rsqrt(jnp.mean(xf * xf, axis=-1, keepdims=True) + eps)).astype(x.dtype)


def causal_depthwise_conv(x, w, b):
    k = w.shape[0]
    y = lax.conv_general_dilated(x, w.astype(x.dtype)[:, None, :], window_strides=(1,),
                                 padding=[(k - 1, 0)], dimension_numbers=('NWC', 'WIO', 'NWC'),
                                 feature_group_count=x.shape[-1])
    return y + b


def segsum(a):
    t = a.shape[-1]
    cs = jnp.cumsum(a, axis=-1)
    diff = cs[..., :, None] - cs[..., None, :]
    return jnp.where(jnp.tril(jnp.ones((t, t), dtype=bool)), diff, -jnp.inf)


def ssd_chunked(x, a, bm, cm):
    bt, s, h, p = x.shape
    g, n = bm.shape[-2:]
    r = h // g
    L = SSD_CHUNK
    nc = s // L
    x = x.reshape(bt, nc, L, g, r, p)
    bm = bm.reshape(bt, nc, L, g, n)
    cm = cm.reshape(bt, nc, L, g, n)
    a = a.astype(jnp.float32).reshape(bt, nc, L, g, r).transpose(0, 3, 4, 1, 2)
    a_cs = jnp.cumsum(a, axis=-1)
    decay = jnp.exp(segsum(a))
    cb = jnp.einsum('bclgn,bcsgn->bgcls', cm, bm)
    y_diag = jnp.einsum('bgcls,bgrcls,bcsgrp->bclgrp', cb, decay, x)
    decay_states = jnp.exp(a_cs[..., -1:] - a_cs)
    states = jnp.einsum('bclgn,bgrcl,bclgrp->bcgrpn', bm, decay_states, x)
    states = jnp.concatenate([jnp.zeros_like(states[:, :1]), states], axis=1)
    chunk_a = jnp.pad(a_cs[..., -1], ((0, 0), (0, 0), (0, 0), (1, 0)))
    chunk_decay = jnp.exp(segsum(chunk_a))
    states = jnp.einsum('bgrzc,bcgrpn->bzgrpn', chunk_decay, states)[:, :-1]
    y_off = jnp.einsum('bclgn,bcgrpn,bgrcl->bclgrp', cm, states, jnp.exp(a_cs))
    return (y_diag + y_off).reshape(bt, s, h * p)


def ssd_branch(z, xbc, dt, conv_w, conv_b, dt_bias, a_log, d_skip, norm_w):
    bt, s, _ = z.shape
    xbc = jax.nn.silu(causal_depthwise_conv(xbc, conv_w, conv_b))
    xs, bm, cm = split_sizes(xbc, (SSD_D_INNER, SSD_N_GROUPS * SSD_D_STATE, SSD_N_GROUPS * SSD_D_STATE))
    xs = xs.reshape(bt, s, SSD_N_HEADS, SSD_HEADDIM)
    bm = bm.reshape(bt, s, SSD_N_GROUPS, SSD_D_STATE)
    cm = cm.reshape(bt, s, SSD_N_GROUPS, SSD_D_STATE)
    dt = jax.nn.softplus(dt.astype(jnp.float32) + dt_bias)
    a_cont = -jnp.exp(a_log.astype(jnp.float32))
    y = ssd_chunked(xs * dt[..., None], dt * a_cont, bm, cm)
    y = y + (xs * d_skip[:, None]).reshape(bt, s, SSD_D_INNER)
    return rms_norm(y * jax.nn.silu(z)) * norm_w


def mlstm_chunked(q, k, v, i_pre, f_pre):
    bt, s, h, dk = q.shape
    dv = v.shape[-1]
    L = MLSTM_CHUNK
    nc = s // L
    q = q.reshape(bt, nc, L, h, dk) * (dk ** -0.5)
    k = k.reshape(bt, nc, L, h, dk)
    v = v.reshape(bt, nc, L, h, dv)
    log_i = i_pre.astype(jnp.float32).reshape(bt, nc, L, h).transpose(0, 3, 1, 2)
    log_f = jax.nn.log_sigmoid(f_pre.astype(jnp.float32)).reshape(bt, nc, L, h).transpose(0, 3, 1, 2)
    b = jnp.cumsum(log_f, axis=-1)
    g = b[..., -1]
    a = g[..., None] - b + log_i
    m_loc = jnp.max(a, axis=-1)
    w = jnp.exp(a - m_loc[..., None])
    c_loc = jnp.einsum('bhcl,bclhk,bclhv->bchkv', w, k, v)
    n_loc = jnp.einsum('bhcl,bclhk->bchk', w, k)

    def step(carry, inp):
        c_prev, n_prev, m_prev = carry
        c_c, n_c, m_c, g_c = inp
        m_new = jnp.maximum(g_c + m_prev, m_c)
        sp = jnp.exp(g_c + m_prev - m_new)
        sc = jnp.exp(m_c - m_new)
        c_new = sp[..., None, None] * c_prev + sc[..., None, None] * c_c
        n_new = sp[..., None] * n_prev + sc[..., None] * n_c
        return (c_new, n_new, m_new), (c_prev, n_prev, m_prev)

    init = (jnp.zeros((bt, h, dk, dv), jnp.float32), jnp.zeros((bt, h, dk), jnp.float32),
            jnp.zeros((bt, h), jnp.float32))
    xs = (c_loc.astype(jnp.float32).transpose(1, 0, 2, 3, 4), n_loc.astype(jnp.float32).transpose(1, 0, 2, 3),
          m_loc.transpose(2, 0, 1), g.transpose(2, 0, 1))
    _, (c_in, n_in, m_in) = lax.scan(step, init, xs)
    d = b[..., :, None] - b[..., None, :] + log_i[..., None, :]
    d = jnp.where(jnp.tril(jnp.ones((L, L), dtype=bool)), d, -jnp.inf)
    m_inter = b + m_in.transpose(1, 2, 0)[..., None]
    m_t = jnp.maximum(m_inter, jnp.max(d, axis=-1))
    scores = jnp.einsum('bclhk,bcshk->bhcls', q, k) * jnp.exp(d - m_t[..., None])
    w_inter = jnp.exp(m_inter - m_t)
    num = (jnp.einsum('bhcls,bcshv->bclhv', scores, v)
           + jnp.einsum('bclhk,cbhkv,bhcl->bclhv', q, c_in, w_inter))
    den = jnp.sum(scores, axis=-1) + jnp.einsum('bclhk,cbhk,bhcl->bhcl', q, n_in, w_inter)
    den = jnp.maximum(jnp.abs(den), jnp.exp(-m_t))
    out = num / den.transpose(0, 2, 3, 1)[..., None]
    return out.reshape(bt, s, h, dv)


def mlstm_branch(qk, v, o, if_pre, conv_w, conv_b, if_bias, norm_w):
    bt, s, _ = v.shape
    qk = jax.nn.silu(causal_depthwise_conv(qk, conv_w, conv_b))
    q, k = jnp.split(qk, 2, axis=-1)
    gates = if_pre + if_bias
    i_pre, f_pre = gates[..., :MLSTM_N_HEADS], gates[..., MLSTM_N_HEADS:]
    h = mlstm_chunked(q.reshape(bt, s, MLSTM_N_HEADS, MLSTM_HEAD_QK),
                      k.reshape(bt, s, MLSTM_N_HEADS, MLSTM_HEAD_QK),
                      v.reshape(bt, s, MLSTM_N_HEADS, MLSTM_HEAD_V), i_pre, f_pre)
    h = rms_norm(h) * norm_w.reshape(MLSTM_N_HEADS, MLSTM_HEAD_V)
    h = jax.nn.sigmoid(o).reshape(bt, s, MLSTM_N_HEADS, MLSTM_HEAD_V) * h
    return h.reshape(bt, s, MLSTM_D_V)


def setup_inputs(seed: int = 0) -> dict:
    key = jax.random.key(seed)
    ks = jax.random.split(key, 24)
    nrm = lambda k, shape, scale: jax.random.normal(k, shape, jnp.float32) * scale
    dt_min, dt_max = 1e-3, 1e-1
    dt0 = jnp.exp(jax.random.uniform(ks[7], (DEPTH, SSD_N_HEADS), jnp.float32)
                  * (np.log(dt_max) - np.log(dt_min)) + np.log(dt_min))
    dt_bias = dt0 + jnp.log(-jnp.expm1(-dt0))
    a_log = jnp.log(jax.random.uniform(ks[8], (DEPTH, SSD_N_HEADS), jnp.float32, 1.0, 16.0))
    i_bias = nrm(ks[12], (DEPTH, MLSTM_N_HEADS), 0.1)
    f_bias = jnp.linspace(3.0, 6.0, MLSTM_N_HEADS, dtype=jnp.float32)[None, :] + nrm(ks[13], (DEPTH, MLSTM_N_HEADS), 0.1)
    return {
        "x": nrm(ks[0], (BATCH, SEQ, D_MODEL), 1.0),
        "c": nrm(ks[1], (BATCH, D_MODEL), 1.0),
        "w_ada": nrm(ks[2], (DEPTH, D_MODEL, 6 * D_MODEL), 0.5 * D_MODEL ** -0.5),
        "b_ada": nrm(ks[3], (DEPTH, 6 * D_MODEL), 0.02),
        "w_in": nrm(ks[4], (DEPTH, D_MODEL, D_IN_PROJ), D_MODEL ** -0.5),
        "ssd_conv_w": nrm(ks[5], (DEPTH, SSD_CONV, SSD_CONV_DIM), SSD_CONV ** -0.5),
        "ssd_conv_b": nrm(ks[6], (DEPTH, SSD_CONV_DIM), 0.02),
        "ssd_dt_bias": dt_bias,
        "ssd_a_log": a_log,
        "ssd_d": 1.0 + nrm(ks[9], (DEPTH, SSD_N_HEADS), 0.02),
        "ssd_norm_w": 1.0 + nrm(ks[10], (DEPTH, SSD_D_INNER), 0.02),
        "mlstm_conv_w": nrm(ks[11], (DEPTH, MLSTM_CONV, 2 * MLSTM_D_QK), MLSTM_CONV ** -0.5),
        "mlstm_conv_b": nrm(ks[14], (DEPTH, 2 * MLSTM_D_QK), 0.02),
        "mlstm_if_bias": jnp.concatenate([i_bias, f_bias], axis=-1),
        "mlstm_norm_w": 1.0 + nrm(ks[15], (DEPTH, MLSTM_D_V), 0.02),
        "w_branch_ssd": nrm(ks[16], (DEPTH, SSD_D_INNER, D_MODEL), SSD_D_INNER ** -0.5),
        "w_branch_mlstm": nrm(ks[17], (DEPTH, MLSTM_D_V, D_MODEL), MLSTM_D_V ** -0.5),
        "w_out": nrm(ks[18], (DEPTH, D_MODEL, D_MODEL), D_MODEL ** -0.5),
        "w_ffn_in": nrm(ks[19], (DEPTH, D_MODEL, 2 * D_FF), D_MODEL ** -0.5),
        "w_ffn_out": nrm(ks[20], (DEPTH, D_FF, D_MODEL), D_FF ** -0.5),
        "final_norm_w": 1.0 + nrm(ks[21], (D_MODEL,), 0.02),
    }


def reference(x, c, w_ada, b_ada, w_in, ssd_conv_w, ssd_conv_b, ssd_dt_bias, ssd_a_log, ssd_d,
              ssd_norm_w, mlstm_conv_w, mlstm_conv_b, mlstm_if_bias, mlstm_norm_w,
              w_branch_ssd, w_branch_mlstm, w_out, w_ffn_in, w_ffn_out, final_norm_w):
    c_act = jax.nn.silu(c)
    for l in range(DEPTH):
        mod = c_act @ w_ada[l] + b_ada[l]
        shift1, scale1, gate1, shift2, scale2, gate2 = [m[:, None, :] for m in jnp.split(mod, 6, axis=-1)]
        h = rms_norm(x) * (1.0 + scale1) + shift1
        proj = h @ w_in[l]
        z, xbc, dt, qk, v, o, if_pre, gate_pre = split_sizes(proj, IN_SPLITS)
        y_ssd = ssd_branch(z, xbc, dt, ssd_conv_w[l], ssd_conv_b[l], ssd_dt_bias[l],
                           ssd_a_log[l], ssd_d[l], ssd_norm_w[l])
        y_mlstm = mlstm_branch(qk, v, o, if_pre, mlstm_conv_w[l], mlstm_conv_b[l],
                               mlstm_if_bias[l], mlstm_norm_w[l])
        g_ssd, g_mlstm = jnp.split(jax.nn.sigmoid(gate_pre), 2, axis=-1)
        merged = g_ssd * (y_ssd @ w_branch_ssd[l]) + g_mlstm * (y_mlstm @ w_branch_mlstm[l])
        x = x + gate1 * (merged @ w_out[l])
        h2 = rms_norm(x) * (1.0 + scale2) + shift2
        gt, up = jnp.split(h2 @ w_ffn_in[l], 2, axis=-1)
        x = x + gate2 * ((jax.nn.silu(gt) * up) @ w_ffn_out[l])
    return rms_norm(x) * final_norm_w
```

```python
import numpy as np
import concourse.bass as bass
import concourse.mybir as mybir
from concourse.bass_utils import run_bass_kernel_spmd

F32 = mybir.dt.float32
BF16 = mybir.dt.bfloat16
AF = mybir.ActivationFunctionType
ALU = mybir.AluOpType

ENGS = ("tensor", "vector", "scalar", "gpsimd", "sync")
D = 1024
NCORES = 8
EPS = 1e-6
O_Z, O_X, O_B, O_C, O_DT, O_Q, O_K, O_V, O_O, O_IF, O_GS, O_GM = (
    0, 2048, 4096, 4608, 5120, 5152, 6176, 7200, 9248, 11296, 11304, 12328)
DFF = 2816
NEG = -30000.0


class Prog:
    def __init__(self, nc):
        self.nc = nc
        self.q = {e: [] for e in ENGS}
        self.cnt = {}
        self.sems = {}
        self.waited = {e: {} for e in ENGS}
        self.last_w = {}
        self.readers = {}
        for e in ENGS:
            self._tl(e)

    def _tl(self, name):
        if name not in self.sems:
            self.sems[name] = self.nc.alloc_semaphore("s_" + name)
            self.cnt[name] = 0
        return self.sems[name]

    def _deps(self, eng, reads, writes):
        deps = []
        for b in reads:
            if b in self.last_w:
                deps.append(self.last_w[b])
        for b in writes:
            if b in self.last_w:
                deps.append(self.last_w[b])
            for tl, c in self.readers.get(b, ()):
                deps.append((tl, c))
        waits = {}
        for tl, c in deps:
            if tl == eng and eng == "tensor":
                continue
            if c > self.waited[eng].get(tl, 0):
                waits[tl] = max(waits.get(tl, 0), c)
        for tl, c in waits.items():
            self.waited[eng][tl] = c
        return waits

    def _commit(self, tl, c, reads, writes):
        for b in writes:
            self.last_w[b] = (tl, c)
            self.readers[b] = []
        for b in reads:
            if b in writes:
                continue
            self.readers.setdefault(b, []).append((tl, c))

    def op(self, eng, fn, reads=(), writes=()):
        reads = list(dict.fromkeys(reads))
        writes = list(dict.fromkeys(writes))
        waits = self._deps(eng, reads, writes)
        self.cnt[eng] += 1
        c = self.cnt[eng]
        self.q[eng].append((waits, fn, (eng, 1)))
        self._commit(eng, c, reads, writes)

    def dma(self, queue, fn, key, reads=(), writes=()):
        self._tl(key)
        waits = self._deps(queue, reads, writes)
        self.cnt[key] += 16
        c = self.cnt[key]
        self.q[queue].append((waits, fn, (key, 16)))
        self._commit(key, c, reads, writes)

    def wait_all(self, eng, keys):
        waits = {}
        for k in keys:
            if k in self.cnt and self.cnt[k] > self.waited[eng].get(k, 0):
                waits[k] = self.cnt[k]
                self.waited[eng][k] = self.cnt[k]
        self.q[eng].append((waits, None, None))

    def emit(self):
        nc = self.nc
        with nc.Block() as block:
            for e in ENGS:
                items = self.q[e]
                if not items:
                    continue

                def body(eng, items=items):
                    for waits, fn, inc in items:
                        for tl, c in waits.items():
                            eng.wait_ge(self.sems[tl], c)
                        if fn is not None:
                            ins = fn(eng)
                            ins.then_inc(self.sems[inc[0]], inc[1])

                getattr(block, e)(body)


def _names(aps):
    out = []
    for a in aps:
        if a is None or isinstance(a, (int, float)):
            continue
        t = a.tensor
        if type(t).__name__ == "DRamTensorHandle" and not t.name.startswith("scr_"):
            continue
        out.append(t.name)
    return out


class KB:
    def __init__(self, SEQ, T, NB=2, dbg=()):
        self.SEQ, self.T, self.NB = SEQ, T, NB
        self.NCH = T // 128
        self.NT = SEQ // T
        self.dbg = set(dbg)
        self.nc = bass.Bass("TRN2", target_bir_lowering=False)
        self.P = Prog(self.nc)
        self.wi = 0
        self.dbg_keys = []
        self.out_keys = []
        self.pa_i = 0
        self.scr = {}
        self.npe = 0
        self.marks = []
        self.use_scr = True

    def sb(self, name, shape, dt=F32):
        return self.nc.alloc_sbuf_tensor(name, list(shape), dt)

    def ps(self, name, shape, dt=F32):
        return self.nc.alloc_psum_tensor(name, list(shape), dt)

    def din(self, name, shape):
        return self.nc.dram_tensor(name, list(shape), F32, kind="ExternalInput").ap()

    def act(self, out, in_, func, bias=None, scale=None, accum_out=None, extra_w=()):
        kw = {}
        if bias is not None:
            kw["bias"] = bias
        if scale is not None:
            kw["scale"] = scale
        if accum_out is not None:
            kw["accum_out"] = accum_out
        self.P.op("scalar", lambda e: e.activation(out=out, in_=in_, func=func, **kw),
                  _names([in_, bias, scale]), _names([out, accum_out]) + list(extra_w))

    def tt(self, out, in0, in1, op, eng="vector"):
        self.P.op(eng, lambda e: e.tensor_tensor(out=out, in0=in0, in1=in1, op=op),
                  _names([in0, in1]), _names([out]))

    def ts(self, out, in0, s1, s2=None, op0=ALU.mult, op1=None, eng="vector"):
        kw = {}
        if op1 is not None:
            kw["op1"] = op1
        self.P.op(eng, lambda e: e.tensor_scalar(out=out, in0=in0, scalar1=s1, scalar2=s2, op0=op0, **kw),
                  _names([in0, s1, s2]), _names([out]))

    def stt(self, out, in0, scalar, in1, op0, op1):
        self.P.op("vector", lambda e: e.scalar_tensor_tensor(out=out, in0=in0, scalar=scalar, in1=in1, op0=op0, op1=op1),
                  _names([in0, scalar, in1]), _names([out]))

    def copy(self, out, in_, eng="vector"):
        if eng == "scalar":
            self.act(out, in_, AF.Copy)
        else:
            self.P.op(eng, lambda e: e.tensor_copy(out=out, in_=in_), _names([in_]), _names([out]))

    def memset(self, ap, val, eng="vector"):
        self.P.op(eng, lambda e: e.memset(ap, val), [], _names([ap]))

    def recip(self, out, in_):
        self.P.op("vector", lambda e: e.reciprocal(out=out, in_=in_), _names([in_]), _names([out]))

    def scan(self, out, d0, d1, init, op0, op1):
        self.P.op("vector", lambda e: e.tensor_tensor_scan(out=out, data0=d0, data1=d1, initial=init, op0=op0, op1=op1),
                  _names([d0, d1, init]), _names([out]))

    def rmax(self, out, in_):
        self.P.op("vector", lambda e: e.tensor_reduce(out=out, in_=in_, axis=mybir.AxisListType.X, op=ALU.max),
                  _names([in_]), _names([out]))

    def asel(self, out, in_, pattern, cmp, fill, base, cm):
        self.P.op("gpsimd", lambda e: e.affine_select(out=out, in_=in_, pattern=pattern, compare_op=cmp, fill=fill,
                                                       base=base, channel_multiplier=cm),
                  _names([in_]), _names([out]))

    def mm(self, items, extra_r=()):
        reads, writes = [], []
        for o, l, r, s, st in items:
            reads += _names([l, r])
            writes += _names([o])

        def fn(e):
            ins = None
            for o, l, r, s, st in items:
                ins = e.matmul(o, lhsT=l, rhs=r, start=s, stop=st)
            return ins
        self.P.op("tensor", fn, reads + list(extra_r), writes)
        self.npe += len(items)

    def mark(self, label):
        self.marks.append((label, self.npe))

    def tr(self, items):
        reads, writes = [], []
        for o, i, idn in items:
            reads += _names([i, idn])
            writes += _names([o])

        def fn(e):
            ins = None
            for o, i, idn in items:
                ins = e.transpose(out=o, in_=i, identity=idn)
            return ins
        self.P.op("tensor", fn, reads, writes)
        self.npe += len(items)

    def dma(self, out, in_, queue="sync", key=None):
        w = _names([out])
        r = _names([in_])
        if key is None:
            key = "d_" + (w[0] if w else r[0])

        def fn(e):
            with self.nc.allow_non_contiguous_dma(reason="small param / strided layouts"):
                return e.dma_start(out=out, in_=in_)
        self.P.dma(queue, fn, key, r, w)
        if not w:
            self.out_keys.append(key)

    def debug(self, name, ap, shape):
        if name not in self.dbg:
            return
        d = self.nc.dram_tensor("dbg_" + name, list(shape), ap.dtype if hasattr(ap, "dtype") else F32,
                                kind="ExternalOutput").ap()
        self.dma(d, ap, key="dd_" + name)
        self.dbg_keys.append("dd_" + name)

    def wload(self, parts, kk, jid, post=None):
        s = self.wi % self.NW
        self.wi += 1
        wt = self.wslots[s]
        n = 4096 // kk
        v = wt[:, 0:kk * n].rearrange("p (k n) -> p k n", k=kk)
        maxc = max(c0 + src.shape[1] for src, c0 in parts)
        assert maxc <= n, (jid, maxc, n)
        if jid not in self.scr:
            for src, c0 in parts:
                ncol = src.shape[1]
                self.dma(v[:, :, c0:c0 + ncol], src.rearrange("(k p) n -> p k n", p=128), queue="gpsimd", key=f"d_w{s}")
            if post is not None:
                post(v)
            if self.use_scr:
                scr = self.nc.dram_tensor("scr_" + jid, [128, kk, maxc], BF16, kind="Internal").ap()
                self.scr[jid] = scr
                self.dma(scr, v[:, :, 0:maxc], queue="sync", key="d_sw_" + jid)
        else:
            self.dma(v[:, :, 0:maxc], self.scr[jid], queue="sync", key=f"d_wq{s}")
        return v

    def pacc(self):
        self.pa_i ^= 1
        return self.pa[self.pa_i]

    def build(self):
        nc, P = self.nc, self.P
        SEQ, T, NB, NCH, NT = self.SEQ, self.T, self.NB, self.NCH, self.NT
        x_d = self.din("x", [NB * SEQ, D])
        c_d = self.din("c", [NB, D])
        w_ada = self.din("w_ada", [D, 6 * D])
        b_ada = self.din("b_ada", [6 * D])
        w_in = self.din("w_in", [D, 13352])
        ssd_conv_w = self.din("ssd_conv_w", [4, 3072])
        ssd_conv_b = self.din("ssd_conv_b", [3072])
        ssd_dt_bias = self.din("ssd_dt_bias", [32])
        ssd_a_log = self.din("ssd_a_log", [32])
        ssd_d = self.din("ssd_d", [32])
        ssd_norm_w = self.din("ssd_norm_w", [2048])
        ml_conv_w = self.din("mlstm_conv_w", [4, 2048])
        ml_conv_b = self.din("mlstm_conv_b", [2048])
        ml_if_bias = self.din("mlstm_if_bias", [8])
        ml_norm_w = self.din("mlstm_norm_w", [2048])
        w_bs = self.din("w_branch_ssd", [2048, D])
        w_bm = self.din("w_branch_mlstm", [2048, D])
        w_out = self.din("w_out", [D, D])
        w_fi = self.din("w_ffn_in", [D, 2 * DFF])
        w_fo = self.din("w_ffn_out", [DFF, D])
        fnw = self.din("final_norm_w", [D])
        out_d = nc.dram_tensor("out", [NB * SEQ, D], F32, kind="ExternalOutput").ap()

        sb, ps = self.sb, self.ps
        self.pa = [ps("pa0", [128, 512]), ps("pa1", [128, 512])]
        pmb = ps("pmb", [128, 1024], BF16)
        pc = ps("pc", [128, 512])
        pd = ps("pd", [128, 512])
        pA = ps("pA", [128, 512])
        pB = ps("pB", [128, 512])
        pS = ps("pS", [128, 512])

        self.NW = 4
        self.wslots = [sb(f"w{i}", [128, 4096], BF16) for i in range(self.NW)]

        ident_f = sb("ident_f", [128, 128]); ident_b = sb("ident_b", [128, 128], BF16)
        ones_f = sb("ones_f", [128, 128]); ones_b = sb("ones_b", [128, 128], BF16)
        onescol_b = sb("onescol_b", [128, 1], BF16)
        negmask = sb("negmask", [128, 512], BF16)
        zeros_b = sb("zeros_b", [128, 512], BF16)
        Ecol = sb("Ecol", [64, 32], BF16); Etmp = sb("Etmp", [64, 32], BF16)
        E4c = sb("E4c", [4, 4]); sel127c = sb("sel127c", [128, 1])
        self.memset(ones_f[:], 1.0)
        self.memset(ones_b[:], 1.0); self.memset(onescol_b[:], 1.0); self.memset(zeros_b[:], 0.0)
        self.asel(ident_f[:], ones_f[:], [[-1, 128]], ALU.is_equal, 0.0, 0, 1)
        self.copy(ident_b[:], ident_f[:])
        self.asel(sel127c[:], ones_f[:, 0:1], [[0, 1]], ALU.is_equal, 0.0, -127, 1)
        self.asel(negmask[:].rearrange("p (j t) -> p j t", j=4), zeros_b[:].rearrange("p (j t) -> p j t", j=4),
                  [[0, 4], [1, 128]], ALU.is_ge, NEG, 0, -1)
        self.asel(Ecol[:], ones_b[0:64, 0:32], [[-1, 32]], ALU.is_equal, 0.0, 0, 1)
        self.asel(Etmp[:], ones_b[0:64, 0:32], [[-1, 32]], ALU.is_equal, 0.0, -32, 1)
        self.tt(Ecol[:], Ecol[:], Etmp[:], ALU.add)
        self.asel(E4c[:], ones_f[0:4, 0:4], [[-1, 4]], ALU.is_equal, 0.0, 0, 1)

        self.raw = sb("raw", [96, 128])
        def rows_to_cols(name, src2d, R):
            raw = self.raw[0:R, :]; dst = sb("p_" + name, [128, R])
            self.dma(raw, src2d)
            self.tr([(pc[:, 0:R], raw, ident_f[0:R, 0:R])])
            self.copy(dst[:], pc[:, 0:R])
            return dst
        cw_s = rows_to_cols("cws", ssd_conv_w.rearrange("j (b p) -> (j b) p", p=128), 96)
        cb_s = rows_to_cols("cbs", ssd_conv_b.rearrange("(b p) -> b p", p=128), 24)
        cw_m = rows_to_cols("cwm", ml_conv_w.rearrange("j (b p) -> (j b) p", p=128), 64)
        cb_m = rows_to_cols("cbm", ml_conv_b.rearrange("(b p) -> b p", p=128), 16)
        mhalf = sb("mhalf", [128, 1])
        self.memset(mhalf[:], -0.5)
        nw_s = rows_to_cols("nws", ssd_norm_w.rearrange("(b p) -> b p", p=128), 16)
        nw_m = rows_to_cols("nwm", ml_norm_w.rearrange("(b p) -> b p", p=128), 16)
        bada = rows_to_cols("bada", b_ada.rearrange("(b p) -> b p", p=128), 48)
        cT = rows_to_cols("cT", c_d.rearrange("b (k p) -> (b k) p", p=128), NB * 8)
        Dcol = sb("Dcol", [128, 16])
        dv = ssd_d.rearrange("(b two) -> two b", two=2)
        self.dma(Dcol[0:64, :], dv[0:1, :].broadcast_to([64, 16]), key="d_Dcol")
        self.dma(Dcol[64:128, :], dv[1:2, :].broadcast_to([64, 16]), key="d_Dcol")
        dtb = sb("dtb", [64, 1]); alog = sb("alog", [64, 1]); Acol = sb("Acol", [64, 1])
        for hh in (0, 32):
            self.dma(dtb[hh:hh + 32, :], ssd_dt_bias.rearrange("(h o) -> h o", o=1), key="d_dtb")
            self.dma(alog[hh:hh + 32, :], ssd_a_log.rearrange("(h o) -> h o", o=1), key="d_alog")
        self.act(Acol[:], alog[:], AF.Exp)
        self.ts(Acol[:], Acol[:], -1.0)
        ib = sb("ib", [4, 1]); fb = sb("fb", [4, 1])
        self.dma(ib[:], ml_if_bias[0:4].rearrange("(h o) -> h o", o=1))
        self.dma(fb[:], ml_if_bias[4:8].rearrange("(h o) -> h o", o=1))
        fnw_bc = sb("fnw_bc", [128, D])
        self.dma(fnw_bc[:], fnw.partition_broadcast(128))
        Dtok = sb("Dtok", [128, 32])
        self.dma(Dtok[:], ssd_d.partition_broadcast(128))
        negE = sb("negE", [64, 32], BF16)
        self.ts(negE[:], Ecol[:], -1.0)

        ca = sb("ca", [128, NB * 8], BF16)
        self.act(ca[:], cT[:], AF.Silu)
        mod = sb("mod", [128, 48, NB])
        gate_bc = sb("gate_bc", [128, 2, D], BF16)
        gate_bias = sb("gate_bias", [128, 512])
        ca_v = ca[:].rearrange("p (b k) -> p b k", b=NB)
        for jb in range(12):
            wv = self.wload([(w_ada[:, jb * 512:(jb + 1) * 512], 0)], 8, f"ada{jb}")
            pm = self.pacc()
            items = []
            for j4 in range(4):
                j = jb * 4 + j4
                for k in range(8):
                    items.append((pm[:, j4 * NB:(j4 + 1) * NB], wv[:, k, j4 * 128:(j4 + 1) * 128], ca_v[:, :, k], k == 0, k == 7))
            self.mm(items)
            self.tt(mod[:, jb * 4:(jb + 1) * 4, :], pm[:, 0:4 * NB].rearrange("p (j b) -> p j b", b=NB),
                    bada[:, jb * 4:(jb + 1) * 4].unsqueeze(2).broadcast_to([128, 4, NB]), ALU.add)
        for blk in (1, 4):
            self.ts(mod[:, blk * 8:(blk + 1) * 8, :], mod[:, blk * 8:(blk + 1) * 8, :], 1.0, op0=ALU.add)
        self.debug("mod", mod[:], [128, 48, NB])

        def make_gate_bc(b):
            for jb, gi in ((4, (0, 0)), (5, (0, 1)), (10, (1, 0)), (11, (1, 1))):
                wv = self.wload([(w_ada[:, jb * 512:(jb + 1) * 512], 0)], 8, f"ada{jb}")
                pg = self.pacc()
                self.mm([(pg[:], ca[:, b * 8 + k:b * 8 + k + 1].broadcast_to([128, 128]), wv[:, k, 0:512], k == 0, k == 7)
                         for k in range(8)])
                self.dma(gate_bias[:], b_ada[jb * 512:(jb + 1) * 512].partition_broadcast(128))
                self.tt(gate_bc[:, gi[0], gi[1] * 512:(gi[1] + 1) * 512], pg[:], gate_bias[:], ALU.add)

        S = [sb(f"S{g}", [128, 512]) for g in range(4)]; Sb = [sb(f"Sb{g}", [128, 512], BF16) for g in range(4)]
        cst = [sb(f"cst{i}", [128, 2, 512]) for i in range(4)]; cb16 = [sb(f"cb16_{i}", [128, 2, 512], BF16) for i in range(4)]
        nst = [sb(f"nst{i}", [128, 2]) for i in range(4)]; nb16 = [sb(f"nb16_{i}", [128, 2], BF16) for i in range(4)]
        mst = sb("mst", [4, 1])
        halo_s = sb("halo_s", [128, 24, 3]); halo_m = sb("halo_m", [128, 16, 3])

        xt = sb("xt", [128, NCH, D])
        junk = sb("junk", [128, D], BF16)
        xn = sb("xn", [128, D], BF16)
        ssq = sb("ssq", [128, NCH]); rstd = sb("rstd", [128, NCH])
        h = sb("h", [128, 8, T], BF16)
        ybuf = sb("ybuf", [128, 32, T], BF16)
        yzw = ybuf[:, 0:16, :]; ymn = ybuf[:, 16:32, :]; actb = ybuf[:, 0:22, :]
        merged = sb("merged", [128, 8, T], BF16)
        pre = [sb(f"pre{i}", [128, T + 3]) for i in range(2)]
        cacc = [sb(f"cacc{i}", [128, T]) for i in range(2)]
        self.pre_i = 0
        g_xs = sb("g_xs", [64, T]); g_ab = sb("g_ab", [64, T]); g_l = sb("g_l", [64, T])
        dtT = sb("dtT", [64, T]); aT = sb("aT", [64, T]); acsT = sb("acsT", [64, T])
        dstT = sb("dstT", [64, T]); eacsT = sb("eacsT", [64, T]); nacsT = sb("nacsT", [64, T])
        hl = sb("hl", [64, T], BF16)
        tok_s = sb("tok_s", [128, NCH, 4, 32])
        etot = sb("etot", [128, NCH, 32])
        gs = sb("gs", [128, 4, T]); gmt = sb("gmt", [128, 4, T], BF16)
        rtmp = sb("rtmp", [128, 512])
        gs_flat = gs[:].rearrange("p a t -> p (a t)")
        ssacc = sb("ssacc", [128, T])

        class Lane:
            pass
        lanes = []
        for i in range(2):
            L = Lane(); L.i = i
            L.sz = sb(f"sz{i}", [128, 4, T], BF16); L.xc = sb(f"xc{i}", [128, 4, T])
            L.BT = sb(f"BT{i}", [128, T], BF16); L.CT = sb(f"CT{i}", [128, T], BF16)
            L.xdt = sb(f"xdt{i}", [128, 512], BF16); L.xdd = sb(f"xdd{i}", [128, 512], BF16)
            L.xD = sb(f"xD{i}", [128, 512])
            L.Btok = sb(f"Btok{i}", [128, 128], BF16); L.CBm = sb(f"CBm{i}", [128, 128], BF16)
            L.MT = sb(f"MT{i}", [128, 4, 128], BF16); L.sq = sb(f"sq{i}", [128, 4, 128], BF16)
            if i == 0:
                L.expD = sb("expD0", [128, 4, 128], BF16)[:]; L.t1 = sb("tmpA", [128, 512])[:]; L.ytok = sb("tmpB", [128, 512])[:]
            else:
                L.expD = sb("expD1", [128, 4, 128], BF16)[:]
                L.t1 = gs_flat[:, 0:512]; L.ytok = gs_flat[:, 512:1024]
            lanes.append(L)
        rstd_bc = sb("rstd_bc", [128, T])
        logi = sb("logi", [4, T]); fx = sb("fx", [4, T]); m_ab = sb("m_ab", [4, T]); m_l = sb("m_l", [4, T])
        logf = sb("logf", [4, T]); bcs = sb("bcs", [4, T]); wv_ = sb("wv_", [4, T]); cmw = sb("cmw", [4, T])
        wloc = sb("wloc", [4, T]); uT = sb("uT", [4, T]); wint = sb("wint", [4, T]); emt = sb("emt", [4, T])
        gch = sb("gch", [4, NCH]); wmax = sb("wmax", [4, NCH]); nwmax = sb("nwmax", [4, NCH]); mloc = sb("mloc", [4, NCH])
        gm = sb("gm", [4, 1]); mnew = sb("mnew", [4, 1]); spsc = sb("spsc", [4, 2]); dg = sb("dg", [4, 8])
        m_in = sb("m_in", [4, NCH])
        ident4 = ident_f[0:4, 0:4]
        mtok = sb("mtok", [128, NCH, 4, 4])
        spsc_bc = sb("spsc_bc", [128, NCH, 8])
        for L in lanes:
            i = L.i
            L.qT = sb(f"qT{i}", [128, 2, T], BF16); L.kT = sb(f"kT{i}", [128, 2, T], BF16)
            if i == 0:
                L.vtok = sb("vtok0", [128, NCH, 512], BF16); L.so = sb("so0", [128, 4, T], BF16)
                L.junk = junk[:, 0:512]
            else:
                mflat = merged[:].rearrange("p a t -> p (a t)")
                L.vtok = mflat[:, 0:NCH * 512].rearrange("p (c v) -> p c v", c=NCH)
                L.so = mflat[:, NCH * 512:NCH * 512 + 4 * T].rearrange("p (a t) -> p a t", a=4)
                L.junk = xn[:, 0:512]
            L.kw = sb(f"kw{i}", [128, 256], BF16)
            L.expDm = sb(f"expDm{i}", [128, 128]); L.ST = sb(f"ST{i}", [128, 128], BF16)
            L.dsm = sb(f"dsm{i}", [128, 4]); L.dden = sb(f"dden{i}", [128, 1]); L.d2 = sb(f"d2_{i}", [128, 1])
            L.hss = sb(f"hss{i}", [128, 1]); L.hrs = sb(f"hrs{i}", [128, 1]); L.ntmp = sb(f"ntmp{i}", [128, 2])
            L.hn = sb(f"hn{i}", [128, 512], BF16)
            L.a1 = L.t1; L.hh = L.ytok; L.ctmp = L.t1
        bt1 = lanes[0].xD[:, 0:T]; bt2 = lanes[1].xD[:, 0:T]
        sgt = lanes[0].t1[:, 0:T]
        ot = gs_flat
        assert NCH * 512 + 4 * T <= 8 * T

        def norm_to_fm(b, shift_j0, scale_j0, dst):
            for c in range(NCH):
                self.act(junk[:], xt[:, c, :], AF.Square, accum_out=ssq[:, c:c + 1])
            self.ts(rstd[:], ssq[:], 1.0 / D, EPS, op0=ALU.mult, op1=ALU.add)
            self.act(rstd[:], rstd[:], AF.Ln)
            self.act(rstd[:], rstd[:], AF.Exp, scale=-0.5)
            for c in range(NCH):
                self.ts(xn[:], xt[:, c, :], rstd[:, c:c + 1])
                self.tr([(pmb[:, k * 128:(k + 1) * 128], xn[:, k * 128:(k + 1) * 128], ident_b[:]) for k in range(8)])
                for k in range(8):
                    self.act(dst[:, k, c * 128:(c + 1) * 128], pmb[:, k * 128:(k + 1) * 128], AF.Identity,
                             scale=mod[:, scale_j0 + k, b:b + 1], bias=mod[:, shift_j0 + k, b:b + 1])

        def proj_fm(wv, col0, ncols, src, kk=8):
            pm = self.pacc()
            self.mm([(pm[0:ncols, 0:T], wv[:, k, col0:col0 + ncols], src[:, k, :], k == 0, k == kk - 1) for k in range(kk)])
            return pm[0:ncols, 0:T]

        def conv_silu(psrc, cw, cb, nblk, bi, halo, first, dst):
            self.pre_i ^= 1
            pr, ac = pre[self.pre_i], cacc[self.pre_i]
            if first:
                self.memset(pr[:, 0:3], 0.0, eng="gpsimd")
            else:
                self.copy(pr[:, 0:3], halo[:, bi, :], eng="gpsimd")
            self.copy(pr[:, 3:3 + T], psrc, eng="scalar")
            self.copy(halo[:, bi, :], pr[:, T:T + 3], eng="gpsimd")
            self.ts(ac[:], pr[:, 3:3 + T], cw[:, 3 * nblk + bi:3 * nblk + bi + 1], cb[:, bi:bi + 1], op0=ALU.mult, op1=ALU.add)
            for j in (2, 1, 0):
                self.stt(ac[:], pr[:, j:j + T], cw[:, j * nblk + bi:j * nblk + bi + 1], ac[:], ALU.mult, ALU.add)
            self.act(dst, ac[:], AF.Silu)

        def softplus_parts(xs_ap, ab_ap, l_ap):
            self.act(ab_ap, xs_ap, AF.Abs)
            self.act(ab_ap, ab_ap, AF.Exp, scale=-1.0)
            self.act(l_ap, ab_ap, AF.Ln, bias=1.0)

        for b in range(NB):
            make_gate_bc(b)
            if b == 0:
                self.debug("gate_bc", gate_bc[:], [128, 2, D])
            for tt_ in range(NT):
                first = (tt_ == 0)
                r0 = b * SEQ + tt_ * T
                self.mark(f"tile{b}_{tt_}:norm1")
                self.dma(xt[:], x_d[r0:r0 + T, :].rearrange("(c p) d -> p c d", p=128))
                norm_to_fm(b, 0, 8, h)
                if b == 0 and tt_ == 0:
                    self.debug("h", h[:], [128, 8, T])
                if first:
                    for i in range(4):
                        self.memset(S[i][:], 0.0); self.memset(Sb[i][:], 0.0, eng="gpsimd")
                        self.memset(cst[i][:], 0.0, eng="gpsimd"); self.memset(cb16[i][:], 0.0)
                        self.memset(nst[i][:], 0.0); self.memset(nb16[i][:], 0.0)
                    self.memset(mst[:], 0.0)

                self.mark("gates")
                wv = self.wload([(w_in[:, O_DT:O_DT + 32], 0), (w_in[:, O_DT:O_DT + 32], 32), (w_in[:, O_IF:O_IF + 8], 64)], 8, "dtif")
                p_dt = proj_fm(wv, 0, 64, h)
                self.act(g_xs[:], p_dt, AF.Identity, bias=dtb[:, 0:1])
                p_i = proj_fm(wv, 64, 4, h)
                self.act(logi[:], p_i, AF.Identity, bias=ib[:, 0:1])
                p_f = proj_fm(wv, 68, 4, h)
                self.act(fx[:], p_f, AF.Identity, bias=fb[:, 0:1])
                softplus_parts(g_xs[:], g_ab[:], g_l[:])
                self.stt(dtT[:], g_xs[:], 0.0, g_l[:], ALU.max, ALU.add)
                self.ts(aT[:], dtT[:], Acol[:, 0:1])
                for c in range(NCH):
                    cs = slice(c * 128, (c + 1) * 128)
                    self.scan(acsT[:, cs], ones_f[0:64, 0:128], aT[:, cs], 0.0, ALU.mult, ALU.add)
                    self.act(dstT[:, cs], acsT[:, cs], AF.Exp, scale=-1.0, bias=acsT[:, c * 128 + 127:c * 128 + 128])
                self.act(eacsT[:], acsT[:], AF.Exp)
                self.ts(nacsT[:], acsT[:], -1.0)
                self.copy(hl[:], acsT[:])
                self.tt(hl[32:64, :], acsT[32:64, :], hl[32:64, :], ALU.subtract)
                for c in range(NCH):
                    cs = slice(c * 128, (c + 1) * 128)
                    self.tr([(pc[:, q * 32:(q + 1) * 32], src[0:32, cs], ident_f[0:32, 0:32])
                             for q, src in enumerate((dtT, nacsT, dstT, eacsT))])
                    self.copy(tok_s[:, c, :, :], pc[:, 0:128].rearrange("p (q h) -> p q h", q=4))
                    self.mm([(pc[:, 128:160], sel127c[:, 0:1].broadcast_to([128, 128]), tok_s[:, c, 3, :], True, True)])
                    self.copy(etot[:, c, :], pc[:, 128:160])
                softplus_parts(fx[:], m_ab[:], m_l[:])
                self.stt(logf[:], fx[:], 0.0, m_l[:], ALU.min, ALU.subtract)
                for c in range(NCH):
                    cs = slice(c * 128, (c + 1) * 128)
                    self.scan(bcs[:, cs], ones_f[0:4, 0:128], logf[:, cs], 0.0, ALU.mult, ALU.add)
                self.tt(wv_[:], logi[:], bcs[:], ALU.subtract)
                for c in range(NCH):
                    cs = slice(c * 128, (c + 1) * 128)
                    self.copy(gch[:, c:c + 1], bcs[:, c * 128 + 127:c * 128 + 128])
                    self.rmax(wmax[:, c:c + 1], wv_[:, cs])
                    self.scan(cmw[:, cs], wv_[:, cs], wv_[:, cs], -1e30, ALU.max, ALU.max)
                self.ts(nwmax[:], wmax[:], -1.0)
                self.tt(mloc[:], gch[:], wmax[:], ALU.add)
                wloc_fix = []
                for c in range(NCH):
                    cs = slice(c * 128, (c + 1) * 128)
                    self.act(wloc[:, cs], wv_[:, cs], AF.Exp, bias=nwmax[:, c:c + 1])
                    wloc_fix.append(c)
                    self.copy(m_in[:, c:c + 1], mst[:])
                    self.tt(gm[:], gch[:, c:c + 1], mst[:], ALU.add)
                    self.tt(mnew[:], gm[:], mloc[:, c:c + 1], ALU.max)
                    self.tt(spsc[:, 0:1], gm[:], mnew[:], ALU.subtract)
                    self.tt(spsc[:, 1:2], mloc[:, c:c + 1], mnew[:], ALU.subtract)
                    self.act(spsc[:], spsc[:], AF.Exp)
                    self.copy(mst[:], mnew[:])
                    self.ts(wloc[:, cs], wloc[:, cs], spsc[:, 1:2])
                    self.ts(dg[:, 0:4], ident4, spsc[:, 0:1])
                    self.ts(dg[:, 4:8], ident4, spsc[:, 1:2])
                    self.mm([(pc[:, 160:168], ones_f[0:4, 0:128], dg[:], True, True)])
                    self.copy(spsc_bc[:, c, :], pc[:, 160:168])
                    self.ts(uT[:, cs], cmw[:, cs], m_in[:, c:c + 1], -1.0, op0=ALU.max, op1=ALU.mult)
                    self.act(wint[:, cs], uT[:, cs], AF.Exp, bias=m_in[:, c:c + 1])
                self.ts(wint[:], wint[:], 0.0625)
                self.tt(emt[:], uT[:], bcs[:], ALU.subtract)
                self.act(emt[:], emt[:], AF.Exp)
                for c in range(NCH):
                    cs = slice(c * 128, (c + 1) * 128)
                    self.tr([(pc[:, 168 + q * 4:168 + (q + 1) * 4], src[0:4, cs], ident_f[0:4, 0:4])
                             for q, src in enumerate((wv_, wloc, wint, emt))])
                    self.copy(mtok[:, c, :, :], pc[:, 168:184].rearrange("p (q h) -> p q h", q=4))
                if b == 0 and tt_ == 0:
                    self.debug("tok_s", tok_s[:], [128, NCH, 4, 32])
                    self.debug("mtok", mtok[:], [128, NCH, 4, 4])
                    self.debug("spsc_bc", spsc_bc[:], [128, NCH, 8])

                self.mark("ssd")
                def interleave(gens):
                    gens = list(gens)
                    while gens:
                        for gg in list(gens):
                            try:
                                next(gg)
                            except StopIteration:
                                gens.remove(gg)

                v8 = lambda ap: ap.rearrange("p (h d) -> p h d", h=8)
                v4 = lambda ap: ap.rearrange("p (q t) -> p q t", q=4)

                def ssd_chain(L, g, P1, P2):
                    P3 = P2
                    i = L.i
                    for c in range(NCH):
                        cs = slice(c * 128, (c + 1) * 128)
                        hs = slice(g * 8, (g + 1) * 8)
                        bc8 = lambda q: tok_s[:, c, q, hs].unsqueeze(2).broadcast_to([128, 8, 64])
                        pmb_i = pmb[:, i * 512:i * 512 + 128]
                        pss_i = pc[:, 256 + i * 128:256 + (i + 1) * 128]
                        self.tr([(pmb_i, L.BT[:, cs], ident_b[:])])
                        self.copy(L.Btok[:], pmb_i, eng="scalar")
                        self.mm([(P2[:, 0:128], L.BT[:, cs], L.CT[:, cs], True, True)])
                        self.copy(L.CBm[:], P2[:, 0:128], eng="scalar")
                        yield
                        self.tr([(P1[:, blk * 128:(blk + 1) * 128], L.xc[:, blk, cs], ident_f[:]) for blk in range(4)])
                        self.tt(v8(L.xdt[:]), v8(P1[:]), bc8(0), ALU.mult)
                        yield
                        self.tt(v8(L.xD[:]), v8(P1[:]), Dtok[:, hs].unsqueeze(2).broadcast_to([128, 8, 64]), ALU.mult)
                        self.tt(v8(L.xdd[:]), v8(L.xdt[:]), bc8(2), ALU.mult, eng="gpsimd")
                        yield
                        for half in range(2):
                            h0 = g * 8 + half * 4
                            items = [(P3[:], ident_b[:], negmask[:], True, False)]
                            for q in range(4):
                                items.append((P3[:, q * 128:(q + 1) * 128], Ecol[:, h0 + q:h0 + q + 1].broadcast_to([64, 128]),
                                              hl[:, cs], False, False))
                            items.append((P3[:], hl[:, cs], negE[:, h0:h0 + 4].unsqueeze(2).broadcast_to([64, 4, 128]), False, True))
                            self.mm(items)
                            yield
                            self.act(L.expD, v4(P3[:]), AF.Exp)
                            yield
                            self.tt(L.MT[:], L.expD, L.CBm[:].unsqueeze(1).broadcast_to([128, 4, 128]), ALU.mult)
                            yield
                            self.mm([(P1[:, (half * 4 + q) * 64:(half * 4 + q + 1) * 64], L.MT[:, q, :],
                                      L.xdt[:, (half * 4 + q) * 64:(half * 4 + q + 1) * 64], True, True) for q in range(4)])
                            yield
                        self.mm([(P2[:], L.CT[:, cs], Sb[g][:], True, True)])
                        self.tt(v8(L.t1), v8(P2[:]), bc8(3), ALU.mult)
                        yield
                        self.tt(L.ytok, L.t1, P1[:], ALU.add)
                        yield
                        self.tt(L.ytok, L.ytok, L.xD[:], ALU.add)
                        self.mm([(P2[:], L.Btok[:], L.xdd[:], True, True)])
                        self.tt(v8(S[g][:]), v8(S[g][:]), etot[:, c, hs].unsqueeze(2).broadcast_to([128, 8, 64]), ALU.mult)
                        yield
                        self.tt(S[g][:], S[g][:], P2[:], ALU.add)
                        self.copy(Sb[g][:], S[g][:], eng="scalar")
                        yield
                        self.tr([(P1[:, blk * 128:(blk + 1) * 128], L.ytok[:, blk * 128:(blk + 1) * 128], ident_f[:]) for blk in range(4)])
                        self.tt(yzw[:, g * 4:(g + 1) * 4, cs], v4(P1[:]), L.sz[:, :, cs], ALU.mult)
                        yield
                        self.act(L.sq[:], yzw[:, g * 4:(g + 1) * 4, cs], AF.Square)
                        yield
                        self.mm([(pss_i, ones_b[:], L.sq[:, blk, :], blk == 0, blk == 3) for blk in range(4)])
                        self.tt(ssacc[:, cs], ssacc[:, cs], pss_i, ALU.add)
                        yield

                def ssd_inproj(gp):
                    for L in lanes:
                        g = gp * 2 + L.i
                        wz = self.wload([(w_in[:, O_Z + g * 512:O_Z + (g + 1) * 512], 0)], 8, f"z{g}")
                        for blk in range(4):
                            self.act(L.sz[:, blk, :], proj_fm(wz, blk * 128, 128, h), AF.Silu)
                            yield
                        wx = self.wload([(w_in[:, O_X + g * 512:O_X + (g + 1) * 512], 0)], 8, f"x{g}")
                        for blk in range(4):
                            conv_silu(proj_fm(wx, blk * 128, 128, h), cw_s, cb_s, 24, g * 4 + blk, halo_s, first, L.xc[:, blk, :])
                            yield
                        wbc = self.wload([(w_in[:, O_B + g * 128:O_B + (g + 1) * 128], 0),
                                          (w_in[:, O_C + g * 128:O_C + (g + 1) * 128], 128)], 8, f"bc{g}")
                        conv_silu(proj_fm(wbc, 0, 128, h), cw_s, cb_s, 24, 16 + g, halo_s, first, L.BT[:])
                        yield
                        conv_silu(proj_fm(wbc, 128, 128, h), cw_s, cb_s, 24, 20 + g, halo_s, first, L.CT[:])
                        yield

                self.mark("mlstm")

                def ml_chain(L, hd, Q1, Q2):
                    i = L.i
                    for c in range(NCH):
                        cs = slice(c * 128, (c + 1) * 128)
                        mt = lambda q: mtok[:, c, q, hd:hd + 1]
                        pk = pmb[:, i * 512:i * 512 + 256]
                        ph = pmb[:, i * 512:(i + 1) * 512]
                        self.tr([(pmb[:, i * 512 + j * 128:i * 512 + (j + 1) * 128], L.kT[:, j, cs], ident_b[:]) for j in range(2)])
                        self.ts(L.kw[:], pk, mt(1))
                        yield
                        self.mm([(Q1[:, 0:128], ident_b[:], negmask[:, 0:128], True, False),
                                 (Q1[:, 0:128], E4c[:, hd:hd + 1].broadcast_to([4, 128]), uT[:, cs], False, True)])
                        self.mm([(Q1[:, 128:256], L.kT[:, j, cs], L.qT[:, j, cs], j == 0, j == 1) for j in range(2)])
                        self.act(L.expDm[:], Q1[:, 0:128], AF.Exp, bias=mt(0))
                        yield
                        self.stt(L.ST[:], Q1[:, 128:256], 0.0625, L.expDm[:], ALU.mult, ALU.mult)
                        yield
                        self.mm([(Q2[:], L.ST[:], L.vtok[:, c, :], True, True)])
                        self.mm([(Q1[:, 256:257], L.ST[:], onescol_b[:], True, True)])
                        self.mm([(Q1[:, 257:258], L.qT[:, j, cs], nb16[hd][:, j:j + 1], j == 0, j == 1) for j in range(2)])
                        yield
                        self.copy(L.a1, Q2[:], eng="scalar")
                        self.copy(L.dsm[:, 0:2], Q1[:, 256:258])
                        yield
                        self.mm([(Q2[:], L.qT[:, j, cs], cb16[hd][:, j, :], j == 0, j == 1) for j in range(2)])
                        yield
                        self.stt(L.hh, Q2[:], mt(2), L.a1, ALU.mult, ALU.add)
                        self.stt(L.dden[:], L.dsm[:, 1:2], mt(2), L.dsm[:, 0:1], ALU.mult, ALU.add)
                        yield
                        self.act(L.junk, L.hh, AF.Square, accum_out=L.hss[:])
                        self.act(L.dden[:], L.dden[:], AF.Abs)
                        yield
                        self.tt(L.dden[:], L.dden[:], mt(3), ALU.max)
                        self.tt(L.d2[:], L.dden[:], L.dden[:], ALU.mult)
                        yield
                        self.ts(L.d2[:], L.d2[:], EPS)
                        self.stt(L.hrs[:], L.hss[:], 1.0 / 512, L.d2[:], ALU.mult, ALU.add)
                        yield
                        self.tt(L.hrs[:], L.hrs[:], mhalf[:], ALU.pow, eng="gpsimd")
                        yield
                        self.ts(L.hn[:], L.hh, L.hrs[:, 0:1])
                        yield
                        self.tr([(pmb[:, i * 512 + blk * 128:i * 512 + (blk + 1) * 128], L.hn[:, blk * 128:(blk + 1) * 128], ident_b[:])
                                 for blk in range(4)])
                        self.tt(ymn[:, hd * 4:(hd + 1) * 4, cs], v4(ph), L.so[:, :, cs], ALU.mult)
                        yield
                        for j in range(2):
                            QQ = Q2
                            self.mm([(QQ[:], L.kw[:, j * 128:(j + 1) * 128], L.vtok[:, c, :], True, True)])
                            self.stt(cst[hd][:, j, :], cst[hd][:, j, :], spsc_bc[:, c, hd:hd + 1], QQ[:], ALU.mult, ALU.add)
                            yield
                            self.copy(cb16[hd][:, j, :], cst[hd][:, j, :], eng="gpsimd")
                            self.mm([(Q1[:, 258 + j:259 + j], L.kw[:, j * 128:(j + 1) * 128], onescol_b[:], True, True)])
                        self.copy(L.dsm[:, 2:4], Q1[:, 258:260])
                        self.stt(nst[hd][:], nst[hd][:], spsc_bc[:, c, hd:hd + 1], L.dsm[:, 2:4], ALU.mult, ALU.add)
                        self.copy(nb16[hd][:], nst[hd][:])
                        yield

                def ml_inproj(hp):
                    for L in lanes:
                        hd = hp * 2 + L.i
                        wqk = self.wload([(w_in[:, O_Q + hd * 256:O_Q + (hd + 1) * 256], 0),
                                          (w_in[:, O_K + hd * 256:O_K + (hd + 1) * 256], 256)], 8, f"qk{hd}")
                        for j in range(2):
                            conv_silu(proj_fm(wqk, j * 128, 128, h), cw_m, cb_m, 16, hd * 2 + j, halo_m, first, L.qT[:, j, :])
                            yield
                            conv_silu(proj_fm(wqk, 256 + j * 128, 128, h), cw_m, cb_m, 16, 8 + hd * 2 + j, halo_m, first, L.kT[:, j, :])
                            yield
                        wvv = self.wload([(w_in[:, O_V + hd * 512:O_V + (hd + 1) * 512], 0)], 8, f"v{hd}")
                        for c in range(NCH):
                            pm = self.pacc()
                            self.mm([(pm[:], h[:, k, c * 128:(c + 1) * 128], wvv[:, k, 0:512], k == 0, k == 7) for k in range(8)])
                            self.copy(L.vtok[:, c, :], pm[:], eng="scalar")
                            yield
                        wo = self.wload([(w_in[:, O_O + hd * 512:O_O + (hd + 1) * 512], 0)], 8, f"o{hd}")
                        for blk in range(4):
                            self.act(L.so[:, blk, :], proj_fm(wo, blk * 128, 128, h), AF.Tanh, scale=0.5)
                            self.ts(L.so[:, blk, :], L.so[:, blk, :], 1.0, op0=ALU.add, eng="gpsimd")
                            yield

                self.memset(ssacc[:], 0.0)
                l0, l1 = lanes
                interleave([ssd_inproj(0)])
                interleave([ssd_chain(l0, 0, pA, pB), ssd_chain(l1, 1, pd, pS), ml_inproj(0)])
                interleave([ml_chain(l0, 0, pA, pB), ml_chain(l1, 1, pd, pS), ssd_inproj(1)])
                interleave([ssd_chain(l0, 2, pA, pB), ssd_chain(l1, 3, pd, pS), ml_inproj(1)])
                self.ts(rstd_bc[:], ssacc[:], 1.0 / 2048, EPS, op0=ALU.mult, op1=ALU.add)
                self.act(rstd_bc[:], rstd_bc[:], AF.Ln)
                self.act(rstd_bc[:], rstd_bc[:], AF.Exp, scale=-0.5)
                interleave([ml_chain(l0, 2, pA, pB), ml_chain(l1, 3, pd, pS)])
                if b == 0 and tt_ == 0:
                    self.debug("ymn", ymn[:], [128, 16, T])

                self.mark("branch")
                bt_all = lanes[0].xc
                for jj in range(2):
                    wgs = self.wload([(w_in[:, O_GS + jj * 512:O_GS + (jj + 1) * 512], 0)], 8, f"gs{jj}")
                    for j4 in range(4):
                        self.act(gs[:, j4, :], proj_fm(wgs, j4 * 128, 128, h), AF.Sigmoid)
                    wgm = self.wload([(w_in[:, O_GM + jj * 512:O_GM + (jj + 1) * 512], 0)], 8, f"gm{jj}")
                    for j4 in range(4):
                        self.act(gmt[:, j4, :], proj_fm(wgm, j4 * 128, 128, h), AF.Sigmoid)
                    self.tt(gs[:], gs[:], rstd_bc[:].unsqueeze(1).broadcast_to([128, 4, T]), ALU.mult)
                    for half in range(2):
                        jq = jj * 2 + half
                        wbs = self.wload([(w_bs[:, jq * 256:(jq + 1) * 256], 0)], 16, f"bs{jq}",
                                          post=lambda v: self.tt(v, v, nw_s[:, 0:16].unsqueeze(2).broadcast_to([128, 16, 256]), ALU.mult))
                        for j2 in range(2):
                            j4 = half * 2 + j2
                            self.tt(bt_all[:, j4, :], proj_fm(wbs, j2 * 128, 128, yzw, kk=16), gs[:, j4, :], ALU.mult)
                    for half in range(2):
                        jq = jj * 2 + half
                        wbm = self.wload([(w_bm[:, jq * 256:(jq + 1) * 256], 0)], 16, f"bm{jq}",
                                          post=lambda v: self.tt(v, v, nw_m[:, 0:16].unsqueeze(2).broadcast_to([128, 16, 256]), ALU.mult))
                        for j2 in range(2):
                            j4 = half * 2 + j2
                            self.stt(bt2[:], proj_fm(wbm, j2 * 128, 128, ymn, kk=16), 0.5, gmt[:, j4, :], ALU.mult, ALU.mult)
                            self.tt(merged[:, jj * 4 + j4, :], bt2[:], bt_all[:, j4, :], ALU.add, eng="gpsimd")
                if b == 0 and tt_ == 0:
                    self.debug("merged", merged[:], [128, 8, T])

                self.mark("wout")
                for jj in range(2):
                    wo_ = self.wload([(w_out[:, jj * 512:(jj + 1) * 512], 0)], 8, f"wo{jj}")
                    for c in range(NCH):
                        pm = self.pacc()
                        self.mm([(pm[:], merged[:, k, c * 128:(c + 1) * 128], wo_[:, k, 0:512], k == 0, k == 7) for k in range(8)])
                        self.tt(rtmp[:], pm[:], gate_bc[:, 0, jj * 512:(jj + 1) * 512], ALU.mult)
                        self.tt(xt[:, c, jj * 512:(jj + 1) * 512], xt[:, c, jj * 512:(jj + 1) * 512], rtmp[:], ALU.add, eng="gpsimd")
                if b == 0 and tt_ == 0:
                    self.debug("x1", xt[:], [128, NCH, D])

                self.mark("ffn_in")
                norm_to_fm(b, 24, 32, h)
                for jj in range(6):
                    nb_ = 4 if jj < 5 else 2
                    wfg = self.wload([(w_fi[:, jj * 512:jj * 512 + nb_ * 128], 0)], 8, f"fig{jj}")
                    wfu = self.wload([(w_fi[:, DFF + jj * 512:DFF + jj * 512 + nb_ * 128], 0)], 8, f"fiu{jj}")
                    for j4 in range(nb_):
                        self.act(sgt[:], proj_fm(wfg, j4 * 128, 128, h), AF.Silu)
                        self.tt(actb[:, jj * 4 + j4, :], proj_fm(wfu, j4 * 128, 128, h), sgt[:], ALU.mult)
                self.mark("ffn_out")
                for cq in range(4):
                    wA = self.wload([(w_fo[0:1408, cq * 256:(cq + 1) * 256], 0)], 11, f"foA{cq}")
                    wB = self.wload([(w_fo[1408:2816, cq * 256:(cq + 1) * 256], 0)], 11, f"foB{cq}")
                    for c in range(NCH):
                        pm = self.pacc()
                        items = []
                        for j in range(22):
                            wsrc = wA if j < 11 else wB
                            items.append((pm[:, 0:256], actb[:, j, c * 128:(c + 1) * 128], wsrc[:, j % 11, 0:256], j == 0, j == 21))
                        self.mm(items)
                        self.tt(rtmp[:, 0:256], pm[:, 0:256], gate_bc[:, 1, cq * 256:(cq + 1) * 256], ALU.mult)
                        self.tt(xt[:, c, cq * 256:(cq + 1) * 256], xt[:, c, cq * 256:(cq + 1) * 256], rtmp[:, 0:256], ALU.add, eng="gpsimd")

                self.mark("final")
                for c in range(NCH):
                    self.act(junk[:], xt[:, c, :], AF.Square, accum_out=ssq[:, c:c + 1])
                self.ts(rstd[:], ssq[:], 1.0 / D, EPS, op0=ALU.mult, op1=ALU.add)
                self.act(rstd[:], rstd[:], AF.Ln)
                self.act(rstd[:], rstd[:], AF.Exp, scale=-0.5)
                for c in range(NCH):
                    self.stt(ot[:], xt[:, c, :], rstd[:, c:c + 1], fnw_bc[:], ALU.mult, ALU.mult)
                    self.dma(out_d[r0 + c * 128:r0 + (c + 1) * 128, :], ot[:], key="d_out")

        P.wait_all("sync", list(dict.fromkeys(self.out_keys + self.dbg_keys)))
        P.emit()
        return nc


_IN_NAMES = ["w_ada", "b_ada", "w_in", "ssd_conv_w", "ssd_conv_b", "ssd_dt_bias", "ssd_a_log", "ssd_d",
             "ssd_norm_w", "mlstm_conv_w", "mlstm_conv_b", "mlstm_if_bias", "mlstm_norm_w",
             "w_branch_ssd", "w_branch_mlstm", "w_out", "w_ffn_in", "w_ffn_out"]


def make_in_maps(inputs, ncores, nb):
    x = np.ascontiguousarray(np.asarray(inputs["x"], dtype=np.float32))
    c = np.ascontiguousarray(np.asarray(inputs["c"], dtype=np.float32))
    seq = x.shape[1]
    shared = {k: np.ascontiguousarray(np.asarray(inputs[k], dtype=np.float32)[0]) for k in _IN_NAMES}
    shared["final_norm_w"] = np.ascontiguousarray(np.asarray(inputs["final_norm_w"], dtype=np.float32))
    maps = []
    for i in range(ncores):
        m = dict(shared)
        m["x"] = x[i * nb:(i + 1) * nb].reshape(nb * seq, D)
        m["c"] = c[i * nb:(i + 1) * nb]
        maps.append(m)
    return maps


def kernel(**inputs):
    x = np.asarray(inputs["x"])
    B, SEQ, _ = x.shape
    nb = B // NCORES
    kb = KB(SEQ=SEQ, T=256, NB=nb)
    nc = kb.build()
    maps = make_in_maps(inputs, NCORES, nb)
    res = run_bass_kernel_spmd(nc, maps, core_ids=list(range(NCORES)))
    outs = [np.asarray(r["out"]).reshape(nb, SEQ, D) for r in res.results]
    return np.concatenate(outs, axis=0).astype(np.float32)
```

```python
import numpy as np
import concourse.bass as bass
import concourse.mybir as mybir
from concourse.bass_utils import run_bass_kernel_spmd

F32 = mybir.dt.float32
BF16 = mybir.dt.bfloat16
AF = mybir.ActivationFunctionType
ALU = mybir.AluOpType

ENGS = ("tensor", "vector", "scalar", "gpsimd", "sync")
D = 1024
NCORES = 8
EPS = 1e-6
O_Z, O_X, O_B, O_C, O_DT, O_Q, O_K, O_V, O_O, O_IF, O_GS, O_GM = (
    0, 2048, 4096, 4608, 5120, 5152, 6176, 7200, 9248, 11296, 11304, 12328)
DFF = 2816
NEG = -30000.0


class Prog:
    def __init__(self, nc):
        self.nc = nc
        self.q = {e: [] for e in ENGS}
        self.cnt = {}
        self.sems = {}
        self.waited = {e: {} for e in ENGS}
        self.last_w = {}
        self.readers = {}
        for e in ENGS:
            self._tl(e)

    def _tl(self, name):
        if name not in self.sems:
            self.sems[name] = self.nc.alloc_semaphore("s_" + name)
            self.cnt[name] = 0
        return self.sems[name]

    def _deps(self, eng, reads, writes):
        deps = []
        for b in reads:
            if b in self.last_w:
                deps.append(self.last_w[b])
        for b in writes:
            if b in self.last_w:
                deps.append(self.last_w[b])
            for tl, c in self.readers.get(b, ()):
                deps.append((tl, c))
        waits = {}
        for tl, c in deps:
            if tl == eng and eng == "tensor":
                continue
            if c > self.waited[eng].get(tl, 0):
                waits[tl] = max(waits.get(tl, 0), c)
        for tl, c in waits.items():
            self.waited[eng][tl] = c
        return waits

    def _commit(self, tl, c, reads, writes):
        for b in writes:
            self.last_w[b] = (tl, c)
            self.readers[b] = []
        for b in reads:
            if b in writes:
                continue
            self.readers.setdefault(b, []).append((tl, c))

    def op(self, eng, fn, reads=(), writes=()):
        reads = list(dict.fromkeys(reads))
        writes = list(dict.fromkeys(writes))
        waits = self._deps(eng, reads, writes)
        self.cnt[eng] += 1
        c = self.cnt[eng]
        self.q[eng].append((waits, fn, (eng, 1)))
        self._commit(eng, c, reads, writes)

    def dma(self, queue, fn, key, reads=(), writes=()):
        self._tl(key)
        waits = self._deps(queue, reads, writes)
        self.cnt[key] += 16
        c = self.cnt[key]
        self.q[queue].append((waits, fn, (key, 16)))
        self._commit(key, c, reads, writes)

    def wait_all(self, eng, keys):
        waits = {}
        for k in keys:
            if k in self.cnt and self.cnt[k] > self.waited[eng].get(k, 0):
                waits[k] = self.cnt[k]
                self.waited[eng][k] = self.cnt[k]
        self.q[eng].append((waits, None, None))

    def emit(self):
        nc = self.nc
        with nc.Block() as block:
            for e in ENGS:
                items = self.q[e]
                if not items:
                    continue

                def body(eng, items=items):
                    for waits, fn, inc in items:
                        for tl, c in waits.items():
                            eng.wait_ge(self.sems[tl], c)
                        if fn is not None:
                            ins = fn(eng)
                            ins.then_inc(self.sems[inc[0]], inc[1])

                getattr(block, e)(body)


def _names(aps):
    out = []
    for a in aps:
        if a is None or isinstance(a, (int, float)):
            continue
        t = a.tensor
        if type(t).__name__ == "DRamTensorHandle" and not t.name.startswith("scr_"):
            continue
        out.append(t.name)
    return out


class KB:
    def __init__(self, SEQ, T, NB=2, dbg=()):
        self.SEQ, self.T, self.NB = SEQ, T, NB
        self.NCH = T // 128
        self.NT = SEQ // T
        self.dbg = set(dbg)
        self.nc = bass.Bass("TRN2", target_bir_lowering=False)
        self.P = Prog(self.nc)
        self.wi = 0
        self.dbg_keys = []
        self.out_keys = []
        self.pa_i = 0
        self.scr = {}
        self.npe = 0
        self.marks = []
        self.use_scr = True

    def sb(self, name, shape, dt=F32):
        return self.nc.alloc_sbuf_tensor(name, list(shape), dt)

    def ps(self, name, shape, dt=F32):
        return self.nc.alloc_psum_tensor(name, list(shape), dt)

    def din(self, name, shape):
        return self.nc.dram_tensor(name, list(shape), F32, kind="ExternalInput").ap()

    def act(self, out, in_, func, bias=None, scale=None, accum_out=None, extra_w=()):
        kw = {}
        if bias is not None:
            kw["bias"] = bias
        if scale is not None:
            kw["scale"] = scale
        if accum_out is not None:
            kw["accum_out"] = accum_out
        self.P.op("scalar", lambda e: e.activation(out=out, in_=in_, func=func, **kw),
                  _names([in_, bias, scale]), _names([out, accum_out]) + list(extra_w))

    def tt(self, out, in0, in1, op, eng="vector"):
        self.P.op(eng, lambda e: e.tensor_tensor(out=out, in0=in0, in1=in1, op=op),
                  _names([in0, in1]), _names([out]))

    def ts(self, out, in0, s1, s2=None, op0=ALU.mult, op1=None, eng="vector"):
        kw = {}
        if op1 is not None:
            kw["op1"] = op1
        self.P.op(eng, lambda e: e.tensor_scalar(out=out, in0=in0, scalar1=s1, scalar2=s2, op0=op0, **kw),
                  _names([in0, s1, s2]), _names([out]))

    def stt(self, out, in0, scalar, in1, op0, op1):
        self.P.op("vector", lambda e: e.scalar_tensor_tensor(out=out, in0=in0, scalar=scalar, in1=in1, op0=op0, op1=op1),
                  _names([in0, scalar, in1]), _names([out]))

    def copy(self, out, in_, eng="vector"):
        if eng == "scalar":
            self.act(out, in_, AF.Copy)
        else:
            self.P.op(eng, lambda e: e.tensor_copy(out=out, in_=in_), _names([in_]), _names([out]))

    def memset(self, ap, val, eng="vector"):
        self.P.op(eng, lambda e: e.memset(ap, val), [], _names([ap]))

    def recip(self, out, in_):
        self.P.op("vector", lambda e: e.reciprocal(out=out, in_=in_), _names([in_]), _names([out]))

    def scan(self, out, d0, d1, init, op0, op1):
        self.P.op("vector", lambda e: e.tensor_tensor_scan(out=out, data0=d0, data1=d1, initial=init, op0=op0, op1=op1),
                  _names([d0, d1, init]), _names([out]))

    def rmax(self, out, in_):
        self.P.op("vector", lambda e: e.tensor_reduce(out=out, in_=in_, axis=mybir.AxisListType.X, op=ALU.max),
                  _names([in_]), _names([out]))

    def asel(self, out, in_, pattern, cmp, fill, base, cm):
        self.P.op("gpsimd", lambda e: e.affine_select(out=out, in_=in_, pattern=pattern, compare_op=cmp, fill=fill,
                                                       base=base, channel_multiplier=cm),
                  _names([in_]), _names([out]))

    def mm(self, items, extra_r=()):
        reads, writes = [], []
        for o, l, r, s, st in items:
            reads += _names([l, r])
            writes += _names([o])

        def fn(e):
            ins = None
            for o, l, r, s, st in items:
                ins = e.matmul(o, lhsT=l, rhs=r, start=s, stop=st)
            return ins
        self.P.op("tensor", fn, reads + list(extra_r), writes)
        self.npe += len(items)

    def mark(self, label):
        self.marks.append((label, self.npe))

    def tr(self, items):
        reads, writes = [], []
        for o, i, idn in items:
            reads += _names([i, idn])
            writes += _names([o])

        def fn(e):
            ins = None
            for o, i, idn in items:
                ins = e.transpose(out=o, in_=i, identity=idn)
            return ins
        self.P.op("tensor", fn, reads, writes)
        self.npe += len(items)

    def dma(self, out, in_, queue="sync", key=None):
        w = _names([out])
        r = _names([in_])
        if key is None:
            key = "d_" + (w[0] if w else r[0])

        def fn(e):
            with self.nc.allow_non_contiguous_dma(reason="small param / strided layouts"):
                return e.dma_start(out=out, in_=in_)
        self.P.dma(queue, fn, key, r, w)
        if not w:
            self.out_keys.append(key)

    def debug(self, name, ap, shape):
        if name not in self.dbg:
            return
        d = self.nc.dram_tensor("dbg_" + name, list(shape), ap.dtype if hasattr(ap, "dtype") else F32,
                                kind="ExternalOutput").ap()
        self.dma(d, ap, key="dd_" + name)
        self.dbg_keys.append("dd_" + name)

    def wload(self, parts, kk, jid, post=None):
        s = self.wi % self.NW
        self.wi += 1
        wt = self.wslots[s]
        n = 4096 // kk
        v = wt[:, 0:kk * n].rearrange("p (k n) -> p k n", k=kk)
        maxc = max(c0 + src.shape[1] for src, c0 in parts)
        assert maxc <= n, (jid, maxc, n)
        if jid not in self.scr:
            for src, c0 in parts:
                ncol = src.shape[1]
                self.dma(v[:, :, c0:c0 + ncol], src.rearrange("(k p) n -> p k n", p=128), queue="gpsimd", key=f"d_w{s}")
            if post is not None:
                post(v)
            if self.use_scr:
                scr = self.nc.dram_tensor("scr_" + jid, [128, kk, maxc], BF16, kind="Internal").ap()
                self.scr[jid] = scr
                self.dma(scr, v[:, :, 0:maxc], queue="sync", key="d_sw_" + jid)
        else:
            self.dma(v[:, :, 0:maxc], self.scr[jid], queue="sync", key=f"d_wq{s}")
        return v

    def pacc(self):
        self.pa_i ^= 1
        return self.pa[self.pa_i]

    def build(self):
        nc, P = self.nc, self.P
        SEQ, T, NB, NCH, NT = self.SEQ, self.T, self.NB, self.NCH, self.NT
        x_d = self.din("x", [NB * SEQ, D])
        c_d = self.din("c", [NB, D])
        w_ada = self.din("w_ada", [D, 6 * D])
        b_ada = self.din("b_ada", [6 * D])
        w_in = self.din("w_in", [D, 13352])
        ssd_conv_w = self.din("ssd_conv_w", [4, 3072])
        ssd_conv_b = self.din("ssd_conv_b", [3072])
        ssd_dt_bias = self.din("ssd_dt_bias", [32])
        ssd_a_log = self.din("ssd_a_log", [32])
        ssd_d = self.din("ssd_d", [32])
        ssd_norm_w = self.din("ssd_norm_w", [2048])
        ml_conv_w = self.din("mlstm_conv_w", [4, 2048])
        ml_conv_b = self.din("mlstm_conv_b", [2048])
        ml_if_bias = self.din("mlstm_if_bias", [8])
        ml_norm_w = self.din("mlstm_norm_w", [2048])
        w_bs = self.din("w_branch_ssd", [2048, D])
        w_bm = self.din("w_branch_mlstm", [2048, D])
        w_out = self.din("w_out", [D, D])
        w_fi = self.din("w_ffn_in", [D, 2 * DFF])
        w_fo = self.din("w_ffn_out", [DFF, D])
        fnw = self.din("final_norm_w", [D])
        out_d = nc.dram_tensor("out", [NB * SEQ, D], F32, kind="ExternalOutput").ap()

        sb, ps = self.sb, self.ps
        self.pa = [ps("pa0", [128, 512]), ps("pa1", [128, 512])]
        pmb = ps("pmb", [128, 1024], BF16)
        pc = ps("pc", [128, 512])
        pd = ps("pd", [128, 512])
        pA = ps("pA", [128, 512])
        pB = ps("pB", [128, 512])
        pS = ps("pS", [128, 512])

        self.NW = 4
        self.wslots = [sb(f"w{i}", [128, 4096], BF16) for i in range(self.NW)]

        ident_f = sb("ident_f", [128, 128]); ident_b = sb("ident_b", [128, 128], BF16)
        ones_f = sb("ones_f", [128, 128]); ones_b = sb("ones_b", [128, 128], BF16)
        onescol_b = sb("onescol_b", [128, 1], BF16)
        negmask = sb("negmask", [128, 512], BF16)
        zeros_b = sb("zeros_b", [128, 512], BF16)
        Ecol = sb("Ecol", [64, 32], BF16); Etmp = sb("Etmp", [64, 32], BF16)
        E4c = sb("E4c", [4, 4]); sel127c = sb("sel127c", [128, 1])
        self.memset(ones_f[:], 1.0)
        self.memset(ones_b[:], 1.0); self.memset(onescol_b[:], 1.0); self.memset(zeros_b[:], 0.0)
        self.asel(ident_f[:], ones_f[:], [[-1, 128]], ALU.is_equal, 0.0, 0, 1)
        self.copy(ident_b[:], ident_f[:])
        self.asel(sel127c[:], ones_f[:, 0:1], [[0, 1]], ALU.is_equal, 0.0, -127, 1)
        self.asel(negmask[:].rearrange("p (j t) -> p j t", j=4), zeros_b[:].rearrange("p (j t) -> p j t", j=4),
                  [[0, 4], [1, 128]], ALU.is_ge, NEG, 0, -1)
        self.asel(Ecol[:], ones_b[0:64, 0:32], [[-1, 32]], ALU.is_equal, 0.0, 0, 1)
        self.asel(Etmp[:], ones_b[0:64, 0:32], [[-1, 32]], ALU.is_equal, 0.0, -32, 1)
        self.tt(Ecol[:], Ecol[:], Etmp[:], ALU.add)
        self.asel(E4c[:], ones_f[0:4, 0:4], [[-1, 4]], ALU.is_equal, 0.0, 0, 1)

        self.raw = sb("raw", [96, 128])
        def rows_to_cols(name, src2d, R):
            raw = self.raw[0:R, :]; dst = sb("p_" + name, [128, R])
            self.dma(raw, src2d)
            self.tr([(pc[:, 0:R], raw, ident_f[0:R, 0:R])])
            self.copy(dst[:], pc[:, 0:R])
            return dst
        cw_s = rows_to_cols("cws", ssd_conv_w.rearrange("j (b p) -> (j b) p", p=128), 96)
        cb_s = rows_to_cols("cbs", ssd_conv_b.rearrange("(b p) -> b p", p=128), 24)
        cw_m = rows_to_cols("cwm", ml_conv_w.rearrange("j (b p) -> (j b) p", p=128), 64)
        cb_m = rows_to_cols("cbm", ml_conv_b.rearrange("(b p) -> b p", p=128), 16)
        mhalf = sb("mhalf", [128, 1])
        self.memset(mhalf[:], -0.5)
        nw_s = rows_to_cols("nws", ssd_norm_w.rearrange("(b p) -> b p", p=128), 16)
        nw_m = rows_to_cols("nwm", ml_norm_w.rearrange("(b p) -> b p", p=128), 16)
        bada = rows_to_cols("bada", b_ada.rearrange("(b p) -> b p", p=128), 48)
        cT = rows_to_cols("cT", c_d.rearrange("b (k p) -> (b k) p", p=128), NB * 8)
        Dcol = sb("Dcol", [128, 16])
        dv = ssd_d.rearrange("(b two) -> two b", two=2)
        self.dma(Dcol[0:64, :], dv[0:1, :].broadcast_to([64, 16]), key="d_Dcol")
        self.dma(Dcol[64:128, :], dv[1:2, :].broadcast_to([64, 16]), key="d_Dcol")
        dtb = sb("dtb", [64, 1]); alog = sb("alog", [64, 1]); Acol = sb("Acol", [64, 1])
        for hh in (0, 32):
            self.dma(dtb[hh:hh + 32, :], ssd_dt_bias.rearrange("(h o) -> h o", o=1), key="d_dtb")
            self.dma(alog[hh:hh + 32, :], ssd_a_log.rearrange("(h o) -> h o", o=1), key="d_alog")
        self.act(Acol[:], alog[:], AF.Exp)
        self.ts(Acol[:], Acol[:], -1.0)
        ib = sb("ib", [4, 1]); fb = sb("fb", [4, 1])
        self.dma(ib[:], ml_if_bias[0:4].rearrange("(h o) -> h o", o=1))
        self.dma(fb[:], ml_if_bias[4:8].rearrange("(h o) -> h o", o=1))
        fnw_bc = sb("fnw_bc", [128, D])
        self.dma(fnw_bc[:], fnw.partition_broadcast(128))
        Dtok = sb("Dtok", [128, 32])
        self.dma(Dtok[:], ssd_d.partition_broadcast(128))
        negE = sb("negE", [64, 32], BF16)
        self.ts(negE[:], Ecol[:], -1.0)

        ca = sb("ca", [128, NB * 8], BF16)
        self.act(ca[:], cT[:], AF.Silu)
        mod = sb("mod", [128, 48, NB])
        gate_bc = sb("gate_bc", [128, 2, D], BF16)
        gate_bias = sb("gate_bias", [128, 512])
        ca_v = ca[:].rearrange("p (b k) -> p b k", b=NB)
        for jb in range(12):
            wv = self.wload([(w_ada[:, jb * 512:(jb + 1) * 512], 0)], 8, f"ada{jb}")
            pm = self.pacc()
            items = []
            for j4 in range(4):
                j = jb * 4 + j4
                for k in range(8):
                    items.append((pm[:, j4 * NB:(j4 + 1) * NB], wv[:, k, j4 * 128:(j4 + 1) * 128], ca_v[:, :, k], k == 0, k == 7))
            self.mm(items)
            self.tt(mod[:, jb * 4:(jb + 1) * 4, :], pm[:, 0:4 * NB].rearrange("p (j b) -> p j b", b=NB),
                    bada[:, jb * 4:(jb + 1) * 4].unsqueeze(2).broadcast_to([128, 4, NB]), ALU.add)
        for blk in (1, 4):
            self.ts(mod[:, blk * 8:(blk + 1) * 8, :], mod[:, blk * 8:(blk + 1) * 8, :], 1.0, op0=ALU.add)
        self.debug("mod", mod[:], [128, 48, NB])

        def make_gate_bc(b):
            for jb, gi in ((4, (0, 0)), (5, (0, 1)), (10, (1, 0)), (11, (1, 1))):
                wv = self.wload([(w_ada[:, jb * 512:(jb + 1) * 512], 0)], 8, f"ada{jb}")
                pg = self.pacc()
                self.mm([(pg[:], ca[:, b * 8 + k:b * 8 + k + 1].broadcast_to([128, 128]), wv[:, k, 0:512], k == 0, k == 7)
                         for k in range(8)])
                self.dma(gate_bias[:], b_ada[jb * 512:(jb + 1) * 512].partition_broadcast(128))
                self.tt(gate_bc[:, gi[0], gi[1] * 512:(gi[1] + 1) * 512], pg[:], gate_bias[:], ALU.add)

        S = [sb(f"S{g}", [128, 512]) for g in range(4)]; Sb = [sb(f"Sb{g}", [128, 512], BF16) for g in range(4)]
        cst = [sb(f"cst{i}", [128, 2, 512]) for i in range(4)]; cb16 = [sb(f"cb16_{i}", [128, 2, 512], BF16) for i in range(4)]
        nst = [sb(f"nst{i}", [128, 2]) for i in range(4)]; nb16 = [sb(f"nb16_{i}", [128, 2], BF16) for i in range(4)]
        mst = sb("mst", [4, 1])
        halo_s = sb("halo_s", [128, 24, 3]); halo_m = sb("halo_m", [128, 16, 3])

        xt = sb("xt", [128, NCH, D])
        junk = sb("junk", [128, D], BF16)
        xn = sb("xn", [128, D], BF16)
        ssq = sb("ssq", [128, NCH]); rstd = sb("rstd", [128, NCH])
        h = sb("h", [128, 8, T], BF16)
        ybuf = sb("ybuf", [128, 32, T], BF16)
        yzw = ybuf[:, 0:16, :]; ymn = ybuf[:, 16:32, :]; actb = ybuf[:, 0:22, :]
        merged = sb("merged", [128, 8, T], BF16)
        pre = [sb(f"pre{i}", [128, T + 3]) for i in range(2)]
        cacc = [sb(f"cacc{i}", [128, T]) for i in range(2)]
        self.pre_i = 0
        g_xs = sb("g_xs", [64, T]); g_ab = sb("g_ab", [64, T]); g_l = sb("g_l", [64, T])
        dtT = sb("dtT", [64, T]); aT = sb("aT", [64, T]); acsT = sb("acsT", [64, T])
        dstT = sb("dstT", [64, T]); eacsT = sb("eacsT", [64, T]); nacsT = sb("nacsT", [64, T])
        hl = sb("hl", [64, T], BF16)
        tok_s = sb("tok_s", [128, NCH, 4, 32])
        etot = sb("etot", [128, NCH, 32])
        gs = sb("gs", [128, 4, T]); gmt = sb("gmt", [128, 4, T], BF16)
        rtmp = sb("rtmp", [128, 512])
        gs_flat = gs[:].rearrange("p a t -> p (a t)")
        ssacc = sb("ssacc", [128, T])

        class Lane:
            pass
        lanes = []
        for i in range(2):
            L = Lane(); L.i = i
            L.sz = sb(f"sz{i}", [128, 4, T], BF16); L.xc = sb(f"xc{i}", [128, 4, T])
            L.BT = sb(f"BT{i}", [128, T], BF16); L.CT = sb(f"CT{i}", [128, T], BF16)
            L.xdt = sb(f"xdt{i}", [128, 512], BF16); L.xdd = sb(f"xdd{i}", [128, 512], BF16)
            L.xD = sb(f"xD{i}", [128, 512])
            L.Btok = sb(f"Btok{i}", [128, 128], BF16); L.CBm = sb(f"CBm{i}", [128, 128], BF16)
            L.MT = sb(f"MT{i}", [128, 4, 128], BF16); L.sq = sb(f"sq{i}", [128, 4, 128], BF16)
            if i == 0:
                L.expD = sb("expD0", [128, 4, 128], BF16)[:]; L.t1 = sb("tmpA", [128, 512])[:]; L.ytok = sb("tmpB", [128, 512])[:]
            else:
                L.expD = sb("expD1", [128, 4, 128], BF16)[:]
                L.t1 = gs_flat[:, 0:512]; L.ytok = gs_flat[:, 512:1024]
            lanes.append(L)
        rstd_bc = sb("rstd_bc", [128, T])
        logi = sb("logi", [4, T]); fx = sb("fx", [4, T]); m_ab = sb("m_ab", [4, T]); m_l = sb("m_l", [4, T])
        logf = sb("logf", [4, T]); bcs = sb("bcs", [4, T]); wv_ = sb("wv_", [4, T]); cmw = sb("cmw", [4, T])
        wloc = sb("wloc", [4, T]); uT = sb("uT", [4, T]); wint = sb("wint", [4, T]); emt = sb("emt", [4, T])
        gch = sb("gch", [4, NCH]); wmax = sb("wmax", [4, NCH]); nwmax = sb("nwmax", [4, NCH]); mloc = sb("mloc", [4, NCH])
        gm = sb("gm", [4, 1]); mnew = sb("mnew", [4, 1]); spsc = sb("spsc", [4, 2]); dg = sb("dg", [4, 8])
        m_in = sb("m_in", [4, NCH])
        ident4 = ident_f[0:4, 0:4]
        mtok = sb("mtok", [128, NCH, 4, 4])
        spsc_bc = sb("spsc_bc", [128, NCH, 8])
        for L in lanes:
            i = L.i
            L.qT = sb(f"qT{i}", [128, 2, T], BF16); L.kT = sb(f"kT{i}", [128, 2, T], BF16)
            if i == 0:
                L.vtok = sb("vtok0", [128, NCH, 512], BF16); L.so = sb("so0", [128, 4, T], BF16)
                L.junk = junk[:, 0:512]
            else:
                mflat = merged[:].rearrange("p a t -> p (a t)")
                L.vtok = mflat[:, 0:NCH * 512].rearrange("p (c v) -> p c v", c=NCH)
                L.so = mflat[:, NCH * 512:NCH * 512 + 4 * T].rearrange("p (a t) -> p a t", a=4)
                L.junk = xn[:, 0:512]
            L.kw = sb(f"kw{i}", [128, 256], BF16)
            L.expDm = sb(f"expDm{i}", [128, 128]); L.ST = sb(f"ST{i}", [128, 128], BF16)
            L.dsm = sb(f"dsm{i}", [128, 4]); L.dden = sb(f"dden{i}", [128, 1]); L.d2 = sb(f"d2_{i}", [128, 1])
            L.hss = sb(f"hss{i}", [128, 1]); L.hrs = sb(f"hrs{i}", [128, 1]); L.ntmp = sb(f"ntmp{i}", [128, 2])
            L.hn = sb(f"hn{i}", [128, 512], BF16)
            L.a1 = L.t1; L.hh = L.ytok; L.ctmp = L.t1
        bt1 = lanes[0].xD[:, 0:T]; bt2 = lanes[1].xD[:, 0:T]
        sgt = lanes[0].t1[:, 0:T]
        ot = gs_flat
        assert NCH * 512 + 4 * T <= 8 * T

        def norm_to_fm(b, shift_j0, scale_j0, dst):
            for c in range(NCH):
                self.act(junk[:], xt[:, c, :], AF.Square, accum_out=ssq[:, c:c + 1])
            self.ts(rstd[:], ssq[:], 1.0 / D, EPS, op0=ALU.mult, op1=ALU.add)
            self.act(rstd[:], rstd[:], AF.Ln)
            self.act(rstd[:], rstd[:], AF.Exp, scale=-0.5)
            for c in range(NCH):
                self.ts(xn[:], xt[:, c, :], rstd[:, c:c + 1])
                self.tr([(pmb[:, k * 128:(k + 1) * 128], xn[:, k * 128:(k + 1) * 128], ident_b[:]) for k in range(8)])
                for k in range(8):
                    self.act(dst[:, k, c * 128:(c + 1) * 128], pmb[:, k * 128:(k + 1) * 128], AF.Identity,
                             scale=mod[:, scale_j0 + k, b:b + 1], bias=mod[:, shift_j0 + k, b:b + 1])

        def proj_fm(wv, col0, ncols, src, kk=8):
            pm = self.pacc()
            self.mm([(pm[0:ncols, 0:T], wv[:, k, col0:col0 + ncols], src[:, k, :], k == 0, k == kk - 1) for k in range(kk)])
            return pm[0:ncols, 0:T]

        def conv_silu(psrc, cw, cb, nblk, bi, halo, first, dst):
            self.pre_i ^= 1
            pr, ac = pre[self.pre_i], cacc[self.pre_i]
            if first:
                self.memset(pr[:, 0:3], 0.0, eng="gpsimd")
            else:
                self.copy(pr[:, 0:3], halo[:, bi, :], eng="gpsimd")
            self.copy(pr[:, 3:3 + T], psrc, eng="scalar")
            self.copy(halo[:, bi, :], pr[:, T:T + 3], eng="gpsimd")
            self.act(ac[:], psrc, AF.Identity, scale=cw[:, 3 * nblk + bi:3 * nblk + bi + 1], bias=cb[:, bi:bi + 1])
            for j in (2, 1, 0):
                self.stt(ac[:], pr[:, j:j + T], cw[:, j * nblk + bi:j * nblk + bi + 1], ac[:], ALU.mult, ALU.add)
            self.act(dst, ac[:], AF.Silu)

        def softplus_parts(xs_ap, ab_ap, l_ap):
            self.act(ab_ap, xs_ap, AF.Abs)
            self.act(ab_ap, ab_ap, AF.Exp, scale=-1.0)
            self.act(l_ap, ab_ap, AF.Ln, bias=1.0)

        for b in range(NB):
            make_gate_bc(b)
            if b == 0:
                self.debug("gate_bc", gate_bc[:], [128, 2, D])
            for tt_ in range(NT):
                first = (tt_ == 0)
                r0 = b * SEQ + tt_ * T
                self.mark(f"tile{b}_{tt_}:norm1")
                self.dma(xt[:], x_d[r0:r0 + T, :].rearrange("(c p) d -> p c d", p=128))
                norm_to_fm(b, 0, 8, h)
                if b == 0 and tt_ == 0:
                    self.debug("h", h[:], [128, 8, T])
                if first:
                    for i in range(4):
                        self.memset(S[i][:], 0.0); self.memset(Sb[i][:], 0.0, eng="gpsimd")
                        self.memset(cst[i][:], 0.0, eng="gpsimd"); self.memset(cb16[i][:], 0.0)
                        self.memset(nst[i][:], 0.0); self.memset(nb16[i][:], 0.0)
                    self.memset(mst[:], 0.0)

                self.mark("gates")
                wv = self.wload([(w_in[:, O_DT:O_DT + 32], 0), (w_in[:, O_DT:O_DT + 32], 32), (w_in[:, O_IF:O_IF + 8], 64)], 8, "dtif")
                p_dt = proj_fm(wv, 0, 64, h)
                self.act(g_xs[:], p_dt, AF.Identity, bias=dtb[:, 0:1])
                p_i = proj_fm(wv, 64, 4, h)
                self.act(logi[:], p_i, AF.Identity, bias=ib[:, 0:1])
                p_f = proj_fm(wv, 68, 4, h)
                self.act(fx[:], p_f, AF.Identity, bias=fb[:, 0:1])
                softplus_parts(g_xs[:], g_ab[:], g_l[:])
                self.stt(dtT[:], g_xs[:], 0.0, g_l[:], ALU.max, ALU.add)
                self.ts(aT[:], dtT[:], Acol[:, 0:1])
                for c in range(NCH):
                    cs = slice(c * 128, (c + 1) * 128)
                    self.scan(acsT[:, cs], ones_f[0:64, 0:128], aT[:, cs], 0.0, ALU.mult, ALU.add)
                    self.act(dstT[:, cs], acsT[:, cs], AF.Exp, scale=-1.0, bias=acsT[:, c * 128 + 127:c * 128 + 128])
                self.act(eacsT[:], acsT[:], AF.Exp)
                self.ts(nacsT[:], acsT[:], -1.0)
                self.copy(hl[:], acsT[:])
                self.tt(hl[32:64, :], acsT[32:64, :], hl[32:64, :], ALU.subtract)
                for c in range(NCH):
                    cs = slice(c * 128, (c + 1) * 128)
                    self.tr([(pc[:, q * 32:(q + 1) * 32], src[0:32, cs], ident_f[0:32, 0:32])
                             for q, src in enumerate((dtT, nacsT, dstT, eacsT))])
                    self.copy(tok_s[:, c, :, :], pc[:, 0:128].rearrange("p (q h) -> p q h", q=4))
                    self.mm([(pc[:, 128:160], sel127c[:, 0:1].broadcast_to([128, 128]), tok_s[:, c, 3, :], True, True)])
                    self.copy(etot[:, c, :], pc[:, 128:160])
                softplus_parts(fx[:], m_ab[:], m_l[:])
                self.stt(logf[:], fx[:], 0.0, m_l[:], ALU.min, ALU.subtract)
                for c in range(NCH):
                    cs = slice(c * 128, (c + 1) * 128)
                    self.scan(bcs[:, cs], ones_f[0:4, 0:128], logf[:, cs], 0.0, ALU.mult, ALU.add)
                self.tt(wv_[:], logi[:], bcs[:], ALU.subtract)
                for c in range(NCH):
                    cs = slice(c * 128, (c + 1) * 128)
                    self.copy(gch[:, c:c + 1], bcs[:, c * 128 + 127:c * 128 + 128])
                    self.rmax(wmax[:, c:c + 1], wv_[:, cs])
                    self.scan(cmw[:, cs], wv_[:, cs], wv_[:, cs], -1e30, ALU.max, ALU.max)
                self.ts(nwmax[:], wmax[:], -1.0)
                self.tt(mloc[:], gch[:], wmax[:], ALU.add)
                wloc_fix = []
                for c in range(NCH):
                    cs = slice(c * 128, (c + 1) * 128)
                    self.act(wloc[:, cs], wv_[:, cs], AF.Exp, bias=nwmax[:, c:c + 1])
                    wloc_fix.append(c)
                    self.copy(m_in[:, c:c + 1], mst[:])
                    self.tt(gm[:], gch[:, c:c + 1], mst[:], ALU.add)
                    self.tt(mnew[:], gm[:], mloc[:, c:c + 1], ALU.max)
                    self.tt(spsc[:, 0:1], gm[:], mnew[:], ALU.subtract)
                    self.tt(spsc[:, 1:2], mloc[:, c:c + 1], mnew[:], ALU.subtract)
                    self.act(spsc[:], spsc[:], AF.Exp)
                    self.copy(mst[:], mnew[:])
                    self.ts(wloc[:, cs], wloc[:, cs], spsc[:, 1:2])
                    self.ts(dg[:, 0:4], ident4, spsc[:, 0:1])
                    self.ts(dg[:, 4:8], ident4, spsc[:, 1:2])
                    self.mm([(pc[:, 160:168], ones_f[0:4, 0:128], dg[:], True, True)])
                    self.copy(spsc_bc[:, c, :], pc[:, 160:168])
                    self.ts(uT[:, cs], cmw[:, cs], m_in[:, c:c + 1], -1.0, op0=ALU.max, op1=ALU.mult)
                    self.act(wint[:, cs], uT[:, cs], AF.Exp, bias=m_in[:, c:c + 1])
                self.ts(wint[:], wint[:], 0.0625)
                self.tt(emt[:], uT[:], bcs[:], ALU.subtract)
                self.act(emt[:], emt[:], AF.Exp)
                for c in range(NCH):
                    cs = slice(c * 128, (c + 1) * 128)
                    self.tr([(pc[:, 168 + q * 4:168 + (q + 1) * 4], src[0:4, cs], ident_f[0:4, 0:4])
                             for q, src in enumerate((wv_, wloc, wint, emt))])
                    self.copy(mtok[:, c, :, :], pc[:, 168:184].rearrange("p (q h) -> p q h", q=4))
                if b == 0 and tt_ == 0:
                    self.debug("tok_s", tok_s[:], [128, NCH, 4, 32])
                    self.debug("mtok", mtok[:], [128, NCH, 4, 4])
                    self.debug("spsc_bc", spsc_bc[:], [128, NCH, 8])

                self.mark("ssd")
                def interleave(gens):
                    gens = list(gens)
                    while gens:
                        for gg in list(gens):
                            try:
                                next(gg)
                            except StopIteration:
                                gens.remove(gg)

                v8 = lambda ap: ap.rearrange("p (h d) -> p h d", h=8)
                v4 = lambda ap: ap.rearrange("p (q t) -> p q t", q=4)

                def ssd_chain(L, g, P1, P2):
                    P3 = P2
                    i = L.i
                    for c in range(NCH):
                        cs = slice(c * 128, (c + 1) * 128)
                        hs = slice(g * 8, (g + 1) * 8)
                        bc8 = lambda q: tok_s[:, c, q, hs].unsqueeze(2).broadcast_to([128, 8, 64])
                        pmb_i = pmb[:, i * 512:i * 512 + 128]
                        pss_i = pc[:, 256 + i * 128:256 + (i + 1) * 128]
                        self.tr([(pmb_i, L.BT[:, cs], ident_b[:])])
                        self.copy(L.Btok[:], pmb_i, eng="scalar")
                        self.mm([(P2[:, 0:128], L.BT[:, cs], L.CT[:, cs], True, True)])
                        self.copy(L.CBm[:], P2[:, 0:128], eng="scalar")
                        yield
                        self.tr([(P1[:, blk * 128:(blk + 1) * 128], L.xc[:, blk, cs], ident_f[:]) for blk in range(4)])
                        self.tt(v8(L.xdt[:]), v8(P1[:]), bc8(0), ALU.mult)
                        yield
                        self.tt(v8(L.xD[:]), v8(P1[:]), Dtok[:, hs].unsqueeze(2).broadcast_to([128, 8, 64]), ALU.mult)
                        self.tt(v8(L.xdd[:]), v8(L.xdt[:]), bc8(2), ALU.mult, eng="gpsimd")
                        yield
                        for half in range(2):
                            h0 = g * 8 + half * 4
                            items = [(P3[:], ident_b[:], negmask[:], True, False)]
                            for q in range(4):
                                items.append((P3[:, q * 128:(q + 1) * 128], Ecol[:, h0 + q:h0 + q + 1].broadcast_to([64, 128]),
                                              hl[:, cs], False, False))
                            items.append((P3[:], hl[:, cs], negE[:, h0:h0 + 4].unsqueeze(2).broadcast_to([64, 4, 128]), False, True))
                            self.mm(items)
                            yield
                            self.act(L.expD, v4(P3[:]), AF.Exp)
                            yield
                            self.tt(L.MT[:], L.expD, L.CBm[:].unsqueeze(1).broadcast_to([128, 4, 128]), ALU.mult)
                            yield
                            self.mm([(P1[:, (half * 4 + q) * 64:(half * 4 + q + 1) * 64], L.MT[:, q, :],
                                      L.xdt[:, (half * 4 + q) * 64:(half * 4 + q + 1) * 64], True, True) for q in range(4)])
                            yield
                        self.mm([(P2[:], L.CT[:, cs], Sb[g][:], True, True)])
                        self.tt(v8(L.t1), v8(P2[:]), bc8(3), ALU.mult)
                        yield
                        self.tt(L.ytok, L.t1, P1[:], ALU.add)
                        yield
                        self.tt(L.ytok, L.ytok, L.xD[:], ALU.add)
                        self.mm([(P2[:], L.Btok[:], L.xdd[:], True, True)])
                        self.tt(v8(S[g][:]), v8(S[g][:]), etot[:, c, hs].unsqueeze(2).broadcast_to([128, 8, 64]), ALU.mult, eng="gpsimd")
                        yield
                        self.tt(S[g][:], S[g][:], P2[:], ALU.add)
                        self.copy(Sb[g][:], S[g][:], eng="scalar")
                        yield
                        self.tr([(P1[:, blk * 128:(blk + 1) * 128], L.ytok[:, blk * 128:(blk + 1) * 128], ident_f[:]) for blk in range(4)])
                        self.tt(yzw[:, g * 4:(g + 1) * 4, cs], v4(P1[:]), L.sz[:, :, cs], ALU.mult)
                        yield
                        self.act(L.sq[:], yzw[:, g * 4:(g + 1) * 4, cs], AF.Square)
                        yield
                        self.mm([(pss_i, ones_b[:], L.sq[:, blk, :], blk == 0, blk == 3) for blk in range(4)])
                        self.tt(ssacc[:, cs], ssacc[:, cs], pss_i, ALU.add)
                        yield

                def ssd_inproj(gp):
                    for L in lanes:
                        g = gp * 2 + L.i
                        wz = self.wload([(w_in[:, O_Z + g * 512:O_Z + (g + 1) * 512], 0)], 8, f"z{g}")
                        for blk in range(4):
                            self.act(L.sz[:, blk, :], proj_fm(wz, blk * 128, 128, h), AF.Silu)
                            yield
                        wx = self.wload([(w_in[:, O_X + g * 512:O_X + (g + 1) * 512], 0)], 8, f"x{g}")
                        for blk in range(4):
                            conv_silu(proj_fm(wx, blk * 128, 128, h), cw_s, cb_s, 24, g * 4 + blk, halo_s, first, L.xc[:, blk, :])
                            yield
                        wbc = self.wload([(w_in[:, O_B + g * 128:O_B + (g + 1) * 128], 0),
                                          (w_in[:, O_C + g * 128:O_C + (g + 1) * 128], 128)], 8, f"bc{g}")
                        conv_silu(proj_fm(wbc, 0, 128, h), cw_s, cb_s, 24, 16 + g, halo_s, first, L.BT[:])
                        yield
                        conv_silu(proj_fm(wbc, 128, 128, h), cw_s, cb_s, 24, 20 + g, halo_s, first, L.CT[:])
                        yield

                self.mark("mlstm")

                def ml_chain(L, hd, Q1, Q2):
                    i = L.i
                    for c in range(NCH):
                        cs = slice(c * 128, (c + 1) * 128)
                        mt = lambda q: mtok[:, c, q, hd:hd + 1]
                        pk = pmb[:, i * 512:i * 512 + 256]
                        ph = pmb[:, i * 512:(i + 1) * 512]
                        self.tr([(pmb[:, i * 512 + j * 128:i * 512 + (j + 1) * 128], L.kT[:, j, cs], ident_b[:]) for j in range(2)])
                        self.act(L.kw[:], pk, AF.Copy, scale=mt(1))
                        yield
                        self.mm([(Q1[:, 0:128], ident_b[:], negmask[:, 0:128], True, False),
                                 (Q1[:, 0:128], E4c[:, hd:hd + 1].broadcast_to([4, 128]), uT[:, cs], False, True)])
                        self.mm([(Q1[:, 128:256], L.kT[:, j, cs], L.qT[:, j, cs], j == 0, j == 1) for j in range(2)])
                        self.act(L.expDm[:], Q1[:, 0:128], AF.Exp, bias=mt(0))
                        yield
                        self.stt(L.ST[:], Q1[:, 128:256], 0.0625, L.expDm[:], ALU.mult, ALU.mult)
                        yield
                        self.mm([(Q2[:], L.ST[:], L.vtok[:, c, :], True, True)])
                        self.mm([(Q1[:, 256:257], L.ST[:], onescol_b[:], True, True)])
                        self.mm([(Q1[:, 257:258], L.qT[:, j, cs], nb16[hd][:, j:j + 1], j == 0, j == 1) for j in range(2)])
                        yield
                        self.copy(L.a1, Q2[:], eng="scalar")
                        self.copy(L.dsm[:, 0:2], Q1[:, 256:258])
                        yield
                        self.mm([(Q2[:], L.qT[:, j, cs], cb16[hd][:, j, :], j == 0, j == 1) for j in range(2)])
                        yield
                        self.stt(L.hh, Q2[:], mt(2), L.a1, ALU.mult, ALU.add)
                        self.stt(L.dden[:], L.dsm[:, 1:2], mt(2), L.dsm[:, 0:1], ALU.mult, ALU.add)
                        yield
                        self.act(L.junk, L.hh, AF.Square, accum_out=L.hss[:])
                        self.act(L.dden[:], L.dden[:], AF.Abs)
                        yield
                        self.tt(L.dden[:], L.dden[:], mt(3), ALU.max)
                        self.tt(L.d2[:], L.dden[:], L.dden[:], ALU.mult)
                        yield
                        self.ts(L.d2[:], L.d2[:], EPS)
                        self.stt(L.hrs[:], L.hss[:], 1.0 / 512, L.d2[:], ALU.mult, ALU.add)
                        yield
                        self.tt(L.hrs[:], L.hrs[:], mhalf[:], ALU.pow, eng="gpsimd")
                        yield
                        self.act(L.hn[:], L.hh, AF.Copy, scale=L.hrs[:, 0:1])
                        yield
                        self.tr([(pmb[:, i * 512 + blk * 128:i * 512 + (blk + 1) * 128], L.hn[:, blk * 128:(blk + 1) * 128], ident_b[:])
                                 for blk in range(4)])
                        self.tt(ymn[:, hd * 4:(hd + 1) * 4, cs], v4(ph), L.so[:, :, cs], ALU.mult)
                        yield
                        for j in range(2):
                            QQ = Q2
                            self.mm([(QQ[:], L.kw[:, j * 128:(j + 1) * 128], L.vtok[:, c, :], True, True)])
                            self.stt(cst[hd][:, j, :], cst[hd][:, j, :], spsc_bc[:, c, hd:hd + 1], QQ[:], ALU.mult, ALU.add)
                            yield
                            self.copy(cb16[hd][:, j, :], cst[hd][:, j, :], eng="gpsimd")
                            self.mm([(Q1[:, 258 + j:259 + j], L.kw[:, j * 128:(j + 1) * 128], onescol_b[:], True, True)])
                        self.copy(L.dsm[:, 2:4], Q1[:, 258:260])
                        self.stt(nst[hd][:], nst[hd][:], spsc_bc[:, c, hd:hd + 1], L.dsm[:, 2:4], ALU.mult, ALU.add)
                        self.copy(nb16[hd][:], nst[hd][:])
                        yield

                def ml_inproj(hp):
                    for L in lanes:
                        hd = hp * 2 + L.i
                        wqk = self.wload([(w_in[:, O_Q + hd * 256:O_Q + (hd + 1) * 256], 0),
                                          (w_in[:, O_K + hd * 256:O_K + (hd + 1) * 256], 256)], 8, f"qk{hd}")
                        for j in range(2):
                            conv_silu(proj_fm(wqk, j * 128, 128, h), cw_m, cb_m, 16, hd * 2 + j, halo_m, first, L.qT[:, j, :])
                            yield
                            conv_silu(proj_fm(wqk, 256 + j * 128, 128, h), cw_m, cb_m, 16, 8 + hd * 2 + j, halo_m, first, L.kT[:, j, :])
                            yield
                        wvv = self.wload([(w_in[:, O_V + hd * 512:O_V + (hd + 1) * 512], 0)], 8, f"v{hd}")
                        for c in range(NCH):
                            pm = self.pacc()
                            self.mm([(pm[:], h[:, k, c * 128:(c + 1) * 128], wvv[:, k, 0:512], k == 0, k == 7) for k in range(8)])
                            self.copy(L.vtok[:, c, :], pm[:], eng="scalar")
                            yield
                        wo = self.wload([(w_in[:, O_O + hd * 512:O_O + (hd + 1) * 512], 0)], 8, f"o{hd}")
                        for blk in range(4):
                            self.act(L.so[:, blk, :], proj_fm(wo, blk * 128, 128, h), AF.Sigmoid)
                            yield

                self.memset(ssacc[:], 0.0)
                l0, l1 = lanes
                interleave([ssd_inproj(0)])
                interleave([ssd_chain(l0, 0, pA, pB), ssd_chain(l1, 1, pd, pS), ml_inproj(0)])
                interleave([ml_chain(l0, 0, pA, pB), ml_chain(l1, 1, pd, pS), ssd_inproj(1)])
                interleave([ssd_chain(l0, 2, pA, pB), ssd_chain(l1, 3, pd, pS), ml_inproj(1)])
                self.ts(rstd_bc[:], ssacc[:], 1.0 / 2048, EPS, op0=ALU.mult, op1=ALU.add)
                self.act(rstd_bc[:], rstd_bc[:], AF.Ln)
                self.act(rstd_bc[:], rstd_bc[:], AF.Exp, scale=-0.5)
                interleave([ml_chain(l0, 2, pA, pB), ml_chain(l1, 3, pd, pS)])
                if b == 0 and tt_ == 0:
                    self.debug("ymn", ymn[:], [128, 16, T])

                self.mark("branch")
                bt_all = lanes[0].xc
                for jj in range(2):
                    wgs = self.wload([(w_in[:, O_GS + jj * 512:O_GS + (jj + 1) * 512], 0)], 8, f"gs{jj}")
                    for j4 in range(4):
                        self.act(gs[:, j4, :], proj_fm(wgs, j4 * 128, 128, h), AF.Sigmoid)
                    wgm = self.wload([(w_in[:, O_GM + jj * 512:O_GM + (jj + 1) * 512], 0)], 8, f"gm{jj}")
                    for j4 in range(4):
                        self.act(gmt[:, j4, :], proj_fm(wgm, j4 * 128, 128, h), AF.Sigmoid)
                    self.tt(gs[:], gs[:], rstd_bc[:].unsqueeze(1).broadcast_to([128, 4, T]), ALU.mult)
                    for half in range(2):
                        jq = jj * 2 + half
                        wbs = self.wload([(w_bs[:, jq * 256:(jq + 1) * 256], 0)], 16, f"bs{jq}",
                                          post=lambda v: self.tt(v, v, nw_s[:, 0:16].unsqueeze(2).broadcast_to([128, 16, 256]), ALU.mult))
                        for j2 in range(2):
                            j4 = half * 2 + j2
                            self.tt(bt_all[:, j4, :], proj_fm(wbs, j2 * 128, 128, yzw, kk=16), gs[:, j4, :], ALU.mult)
                    for half in range(2):
                        jq = jj * 2 + half
                        wbm = self.wload([(w_bm[:, jq * 256:(jq + 1) * 256], 0)], 16, f"bm{jq}",
                                          post=lambda v: self.tt(v, v, nw_m[:, 0:16].unsqueeze(2).broadcast_to([128, 16, 256]), ALU.mult))
                        for j2 in range(2):
                            j4 = half * 2 + j2
                            self.tt(bt2[:], proj_fm(wbm, j2 * 128, 128, ymn, kk=16), gmt[:, j4, :], ALU.mult)
                            self.tt(merged[:, jj * 4 + j4, :], bt2[:], bt_all[:, j4, :], ALU.add, eng="gpsimd")
                if b == 0 and tt_ == 0:
                    self.debug("merged", merged[:], [128, 8, T])

                self.mark("wout")
                for jj in range(2):
                    wo_ = self.wload([(w_out[:, jj * 512:(jj + 1) * 512], 0)], 8, f"wo{jj}")
                    for c in range(NCH):
                        pm = self.pacc()
                        self.mm([(pm[:], merged[:, k, c * 128:(c + 1) * 128], wo_[:, k, 0:512], k == 0, k == 7) for k in range(8)])
                        self.tt(rtmp[:], pm[:], gate_bc[:, 0, jj * 512:(jj + 1) * 512], ALU.mult)
                        self.tt(xt[:, c, jj * 512:(jj + 1) * 512], xt[:, c, jj * 512:(jj + 1) * 512], rtmp[:], ALU.add, eng="gpsimd")
                if b == 0 and tt_ == 0:
                    self.debug("x1", xt[:], [128, NCH, D])

                self.mark("ffn_in")
                norm_to_fm(b, 24, 32, h)
                for jj in range(6):
                    nb_ = 4 if jj < 5 else 2
                    wfg = self.wload([(w_fi[:, jj * 512:jj * 512 + nb_ * 128], 0)], 8, f"fig{jj}")
                    wfu = self.wload([(w_fi[:, DFF + jj * 512:DFF + jj * 512 + nb_ * 128], 0)], 8, f"fiu{jj}")
                    for j4 in range(nb_):
                        self.act(sgt[:], proj_fm(wfg, j4 * 128, 128, h), AF.Silu)
                        self.tt(actb[:, jj * 4 + j4, :], proj_fm(wfu, j4 * 128, 128, h), sgt[:], ALU.mult)
                self.mark("ffn_out")
                for cq in range(4):
                    wA = self.wload([(w_fo[0:1408, cq * 256:(cq + 1) * 256], 0)], 11, f"foA{cq}")
                    wB = self.wload([(w_fo[1408:2816, cq * 256:(cq + 1) * 256], 0)], 11, f"foB{cq}")
                    for c in range(NCH):
                        pm = self.pacc()
                        items = []
                        for j in range(22):
                            wsrc = wA if j < 11 else wB
                            items.append((pm[:, 0:256], actb[:, j, c * 128:(c + 1) * 128], wsrc[:, j % 11, 0:256], j == 0, j == 21))
                        self.mm(items)
                        self.tt(rtmp[:, 0:256], pm[:, 0:256], gate_bc[:, 1, cq * 256:(cq + 1) * 256], ALU.mult)
                        self.tt(xt[:, c, cq * 256:(cq + 1) * 256], xt[:, c, cq * 256:(cq + 1) * 256], rtmp[:, 0:256], ALU.add, eng="gpsimd")

                self.mark("final")
                for c in range(NCH):
                    self.act(junk[:], xt[:, c, :], AF.Square, accum_out=ssq[:, c:c + 1])
                self.ts(rstd[:], ssq[:], 1.0 / D, EPS, op0=ALU.mult, op1=ALU.add)
                self.act(rstd[:], rstd[:], AF.Ln)
                self.act(rstd[:], rstd[:], AF.Exp, scale=-0.5)
                for c in range(NCH):
                    self.stt(ot[:], xt[:, c, :], rstd[:, c:c + 1], fnw_bc[:], ALU.mult, ALU.mult)
                    self.dma(out_d[r0 + c * 128:r0 + (c + 1) * 128, :], ot[:], key="d_out")

        P.wait_all("sync", list(dict.fromkeys(self.out_keys + self.dbg_keys)))
        P.emit()
        return nc


_IN_NAMES = ["w_ada", "b_ada", "w_in", "ssd_conv_w", "ssd_conv_b", "ssd_dt_bias", "ssd_a_log", "ssd_d",
             "ssd_norm_w", "mlstm_conv_w", "mlstm_conv_b", "mlstm_if_bias", "mlstm_norm_w",
             "w_branch_ssd", "w_branch_mlstm", "w_out", "w_ffn_in", "w_ffn_out"]


def make_in_maps(inputs, ncores, nb):
    x = np.ascontiguousarray(np.asarray(inputs["x"], dtype=np.float32))
    c = np.ascontiguousarray(np.asarray(inputs["c"], dtype=np.float32))
    seq = x.shape[1]
    shared = {k: np.ascontiguousarray(np.asarray(inputs[k], dtype=np.float32)[0]) for k in _IN_NAMES}
    shared["final_norm_w"] = np.ascontiguousarray(np.asarray(inputs["final_norm_w"], dtype=np.float32))
    maps = []
    for i in range(ncores):
        m = dict(shared)
        m["x"] = x[i * nb:(i + 1) * nb].reshape(nb * seq, D)
        m["c"] = c[i * nb:(i + 1) * nb]
        maps.append(m)
    return maps


def kernel(**inputs):
    x = np.asarray(inputs["x"])
    B, SEQ, _ = x.shape
    nb = B // NCORES
    kb = KB(SEQ=SEQ, T=256, NB=nb)
    nc = kb.build()
    maps = make_in_maps(inputs, NCORES, nb)
    res = run_bass_kernel_spmd(nc, maps, core_ids=list(range(NCORES)))
    outs = [np.asarray(r["out"]).reshape(nb, SEQ, D) for r in res.results]
    return np.concatenate(outs, axis=0).astype(np.float32)
```
